# Optimizing a Trainium2 kernel written in Bass

```python
import math
import jax, jax.numpy as jnp
from jax import lax
import numpy as np

D_MODEL = 1024
BATCH = 16
SEQ = 2048
DEPTH = 1

EPS = 1e-6
D_RNN = 512
RNN_HEADS = 8
RNN_HEAD_DIM = D_RNN // RNN_HEADS
CONV_WIDTH = 4
LRU_C = 8.0
LRU_A_MIN = 0.9
LRU_A_MAX = 0.999
N_Q_HEADS = 8
N_KV_HEADS = 2
HEAD_DIM = 64
Q_PER_KV = N_Q_HEADS // N_KV_HEADS
D_ATTN = N_Q_HEADS * HEAD_DIM
D_KV = N_KV_HEADS * HEAD_DIM
CMP_BLOCK = 32
CMP_STRIDE = 16
CMP_HIDDEN = 256
SEL_BLOCK = 64
SEL_TOPK = 8
SEL_Q_BLOCK = 64
SEL_FORCE = 1e9
WINDOW = 512
WIN_Q_BLOCK = 128
N_BRANCH = 3
D_MIX = D_RNN + D_ATTN
SPLIT_SIZES = (D_RNN, D_RNN, D_ATTN, D_KV, D_KV, D_KV, D_KV, D_KV, D_KV, D_ATTN, N_BRANCH * N_Q_HEADS)
D_IN = 2 * D_RNN + 2 * D_ATTN + 6 * D_KV + N_BRANCH * N_Q_HEADS

kernel_name = "hybrid_rglru_nsa_parallel_heads"


def rms_norm(x, w):
    xf = x.astype(jnp.float32)
    y = xf * lax.rsqrt(jnp.mean(xf * xf, axis=-1, keepdims=True) + EPS)
    return (y * w.astype(jnp.float32)).astype(x.dtype)


def masked_softmax(s, mask):
    s = jnp.where(mask, s, -jnp.inf)
    m = jnp.max(s, axis=-1, keepdims=True)
    m = jnp.where(jnp.isfinite(m), m, 0.0)
    e = jnp.where(mask, jnp.exp(s - m), 0.0)
    den = jnp.sum(e, axis=-1, keepdims=True)
    return e / jnp.where(den > 0, den, 1.0)


def causal_depthwise_conv(x, w, b):
    S = x.shape[1]
    xp = jnp.pad(x, ((0, 0), (CONV_WIDTH - 1, 0), (0, 0)))
    out = b
    for k in range(CONV_WIDTH):
        out = out + xp[:, k:k + S] * w[k]
    return out


def rg_lru(x, wa, ba, wx, bx, lam):
    B, S, _ = x.shape
    xh = x.reshape(B, S, RNN_HEADS, RNN_HEAD_DIM)
    r = jax.nn.sigmoid(jnp.einsum('bshi,hij->bshj', xh, wa).reshape(B, S, D_RNN) + ba)
    i = jax.nn.sigmoid(jnp.einsum('bshi,hij->bshj', xh, wx).reshape(B, S, D_RNN) + bx)
    log_a = LRU_C * r.astype(jnp.float32) * jax.nn.log_sigmoid(lam.astype(jnp.float32))
    a = jnp.exp(log_a)
    u = jnp.sqrt(-jnp.expm1(2.0 * log_a)) * (i * x).astype(jnp.float32)

    def combine(c1, c2):
        a1, b1 = c1
        a2, b2 = c2
        return a1 * a2, a2 * b1 + b2

    _, h = lax.associative_scan(combine, (a, u), axis=1)
    return h.astype(x.dtype)


def compress_blocks(k, pe, w1, w2):
    B, S, G, dh = k.shape
    n_cmp = (S - CMP_BLOCK) // CMP_STRIDE + 1
    idx = jnp.arange(n_cmp)[:, None] * CMP_STRIDE + jnp.arange(CMP_BLOCK)[None, :]
    blocks = k[:, idx] + pe[None, None, :, None, :]
    flat = blocks.transpose(0, 1, 3, 2, 4).reshape(B, n_cmp, G, CMP_BLOCK * dh)
    return jax.nn.silu(flat @ w1) @ w2


def cmp_sel_overlap(n_cmp, n_sel):
    cs = jnp.arange(n_cmp)[:, None] * CMP_STRIDE
    ss = jnp.arange(n_sel)[None, :] * SEL_BLOCK
    ov = jnp.clip(jnp.minimum(cs + CMP_BLOCK, ss + SEL_BLOCK) - jnp.maximum(cs, ss), 0, None)
    return ov.astype(jnp.float32) / CMP_BLOCK


def to_chunks(a, size):
    B, S = a.shape[:2]
    return jnp.moveaxis(a.reshape((B, S // size, size) + a.shape[2:]), 1, 0)


def selected_attention(q, k, v, idx, valid):
    B, S, G, R, dh = q.shape
    n = idx.shape[-1]
    scale = dh ** -0.5
    kb = k.reshape(B, S // SEL_BLOCK, SEL_BLOCK, G, dh).transpose(0, 3, 1, 2, 4)
    vb = v.reshape(B, S // SEL_BLOCK, SEL_BLOCK, G, dh).transpose(0, 3, 1, 2, 4)
    gather = jax.vmap(jax.vmap(lambda blocks, ids: blocks[ids]))

    def chunk(args):
        q_c, idx_c, valid_c, t_c = args
        Tq = q_c.shape[1]
        ids = idx_c.transpose(0, 2, 1, 3)
        kg = gather(kb, ids)
        vg = gather(vb, ids)
        pos = idx_c[..., None] * SEL_BLOCK + jnp.arange(SEL_BLOCK)
        mask = valid_c[..., None] & (pos <= t_c[None, :, None, None, None])
        s = jnp.einsum('btgrd,bgtnld->btgrnl', q_c, kg).astype(jnp.float32) * scale
        s = s.reshape(B, Tq, G, R, n * SEL_BLOCK)
        p = masked_softmax(s, mask.reshape(B, Tq, G, 1, n * SEL_BLOCK)).reshape(B, Tq, G, R, n, SEL_BLOCK)
        return jnp.einsum('btgrnl,bgtnld->btgrd', p.astype(vg.dtype), vg)

    t_chunks = jnp.arange(S).reshape(S // SEL_Q_BLOCK, SEL_Q_BLOCK)
    out = lax.map(chunk, (to_chunks(q, SEL_Q_BLOCK), to_chunks(idx, SEL_Q_BLOCK),
                          to_chunks(valid, SEL_Q_BLOCK), t_chunks))
    return jnp.moveaxis(out, 0, 1).reshape(B, S, G, R, dh)


def window_attention(q, k, v):
    B, S, G, R, dh = q.shape
    scale = dh ** -0.5
    kp = jnp.pad(k, ((0, 0), (WINDOW, 0), (0, 0), (0, 0)))
    vp = jnp.pad(v, ((0, 0), (WINDOW, 0), (0, 0), (0, 0)))
    band = WIN_Q_BLOCK + WINDOW

    def block(i):
        q0 = i * WIN_Q_BLOCK
        qb = lax.dynamic_slice_in_dim(q, q0, WIN_Q_BLOCK, axis=1)
        kb = lax.dynamic_slice_in_dim(kp, q0, band, axis=1)
        vb = lax.dynamic_slice_in_dim(vp, q0, band, axis=1)
        t = q0 + jnp.arange(WIN_Q_BLOCK)
        s_pos = q0 - WINDOW + jnp.arange(band)
        mask = (s_pos[None, :] <= t[:, None]) & (s_pos[None, :] > t[:, None] - WINDOW) & (s_pos[None, :] >= 0)
        s = jnp.einsum('btgrd,bsgd->btgrs', qb, kb).astype(jnp.float32) * scale
        p = masked_softmax(s, mask[None, :, None, None, :])
        return jnp.einsum('btgrs,bsgd->btgrd', p.astype(vb.dtype), vb)

    out = lax.map(block, jnp.arange(S // WIN_Q_BLOCK))
    return jnp.moveaxis(out, 0, 1).reshape(B, S, G, R, dh)


def nsa_attention(q, k_cmp, v_cmp, k_sel, v_sel, k_win, v_win, br_gate,
                  cmp_k_pe, cmp_k_w1, cmp_k_w2, cmp_v_pe, cmp_v_w1, cmp_v_w2):
    B, S, _ = q.shape
    G, R, dh = N_KV_HEADS, Q_PER_KV, HEAD_DIM
    scale = dh ** -0.5
    q = q.reshape(B, S, G, R, dh)
    kv = lambda a: a.reshape(B, S, G, dh)
    t_pos = jnp.arange(S)
    kc = compress_blocks(kv(k_cmp), cmp_k_pe, cmp_k_w1, cmp_k_w2)
    vc = compress_blocks(kv(v_cmp), cmp_v_pe, cmp_v_w1, cmp_v_w2)
    n_cmp = kc.shape[1]
    cmp_end = jnp.arange(n_cmp) * CMP_STRIDE + CMP_BLOCK - 1
    cmp_mask = cmp_end[None, :] <= t_pos[:, None]
    s_cmp = jnp.einsum('btgrd,bcgd->btgrc', q, kc).astype(jnp.float32) * scale
    p_cmp = masked_softmax(s_cmp, cmp_mask[None, :, None, None, :])
    o_cmp = jnp.einsum('btgrc,bcgd->btgrd', p_cmp.astype(vc.dtype), vc)
    n_sel_blocks = S // SEL_BLOCK
    imp = jnp.einsum('btgrc,cj->btgj', p_cmp, cmp_sel_overlap(n_cmp, n_sel_blocks))
    blk = jnp.arange(n_sel_blocks)[None, :]
    cur = (t_pos // SEL_BLOCK)[:, None]
    forced = (blk == 0) | (blk == cur) | (blk == cur - 1)
    causal_blk = blk * SEL_BLOCK <= t_pos[:, None]
    imp = jnp.where(forced[None, :, None, :], SEL_FORCE, imp)
    imp = jnp.where(causal_blk[None, :, None, :], imp, -SEL_FORCE)
    top_val, top_idx = lax.top_k(imp, min(SEL_TOPK, n_sel_blocks))
    top_valid = top_val > -0.5 * SEL_FORCE
    o_sel = selected_attention(q, kv(k_sel), kv(v_sel), top_idx, top_valid)
    o_win = window_attention(q, kv(k_win), kv(v_win))
    g = jax.nn.sigmoid(br_gate.astype(jnp.float32)).reshape(B, S, N_BRANCH, G, R)[..., None]
    o = g[:, :, 0] * o_cmp + g[:, :, 1] * o_sel + g[:, :, 2] * o_win
    return o.reshape(B, S, D_ATTN).astype(br_gate.dtype)


def setup_inputs(seed: int = 0) -> dict:
    key = jax.random.key(seed)
    ks = jax.random.split(key, 20)
    f32 = jnp.float32
    L = DEPTH

    def nrm(k, shape, scale):
        return jax.random.normal(k, shape, f32) * scale

    u = jax.random.uniform(ks[9], (L, D_RNN), f32, LRU_A_MIN ** (1.0 / LRU_C), LRU_A_MAX ** (1.0 / LRU_C))
    return {
        "x": nrm(ks[0], (BATCH, SEQ, D_MODEL), 1.0),
        "norm1_w": 1.0 + nrm(ks[1], (L, D_MODEL), 0.02),
        "w_in": nrm(ks[2], (L, D_MODEL, D_IN), D_MODEL ** -0.5),
        "conv_w": nrm(ks[3], (L, CONV_WIDTH, D_RNN), CONV_WIDTH ** -0.5),
        "conv_b": nrm(ks[4], (L, D_RNN), 0.01),
        "rg_wa": nrm(ks[5], (L, RNN_HEADS, RNN_HEAD_DIM, RNN_HEAD_DIM), RNN_HEAD_DIM ** -0.5),
        "rg_ba": nrm(ks[6], (L, D_RNN), 0.01),
        "rg_wx": nrm(ks[7], (L, RNN_HEADS, RNN_HEAD_DIM, RNN_HEAD_DIM), RNN_HEAD_DIM ** -0.5),
        "rg_bx": nrm(ks[8], (L, D_RNN), 0.01),
        "rg_lambda": jnp.log(u) - jnp.log1p(-u),
        "cmp_k_pe": nrm(ks[10], (L, CMP_BLOCK, HEAD_DIM), 0.1),
        "cmp_k_w1": nrm(ks[11], (L, CMP_BLOCK * HEAD_DIM, CMP_HIDDEN), (CMP_BLOCK * HEAD_DIM) ** -0.5),
        "cmp_k_w2": nrm(ks[12], (L, CMP_HIDDEN, HEAD_DIM), CMP_HIDDEN ** -0.5),
        "cmp_v_pe": nrm(ks[13], (L, CMP_BLOCK, HEAD_DIM), 0.1),
        "cmp_v_w1": nrm(ks[14], (L, CMP_BLOCK * HEAD_DIM, CMP_HIDDEN), (CMP_BLOCK * HEAD_DIM) ** -0.5),
        "cmp_v_w2": nrm(ks[15], (L, CMP_HIDDEN, HEAD_DIM), CMP_HIDDEN ** -0.5),
        "w_out": nrm(ks[16], (L, D_MIX, D_MODEL), D_MIX ** -0.5),
        "normf_w": 1.0 + nrm(ks[17], (D_MODEL,), 0.02),
    }


def reference(x, norm1_w, w_in, conv_w, conv_b, rg_wa, rg_ba, rg_wx, rg_bx, rg_lambda,
              cmp_k_pe, cmp_k_w1, cmp_k_w2, cmp_v_pe, cmp_v_w1, cmp_v_w2, w_out, normf_w):
    split_at = [int(v) for v in np.cumsum(SPLIT_SIZES)[:-1]]
    for l in range(DEPTH):
        h = rms_norm(x, norm1_w[l])
        proj = h @ w_in[l]
        (rnn_x, rnn_gate, q, k_cmp, v_cmp, k_sel, v_sel, k_win, v_win,
         attn_gate, br_gate) = jnp.split(proj, split_at, axis=-1)
        rnn_in = causal_depthwise_conv(rnn_x, conv_w[l], conv_b[l])
        rnn_out = rg_lru(rnn_in, rg_wa[l], rg_ba[l], rg_wx[l], rg_bx[l], rg_lambda[l]) * jax.nn.silu(rnn_gate)
        attn = nsa_attention(q, k_cmp, v_cmp, k_sel, v_sel, k_win, v_win, br_gate,
                             cmp_k_pe[l], cmp_k_w1[l], cmp_k_w2[l], cmp_v_pe[l], cmp_v_w1[l], cmp_v_w2[l])
        attn_out = attn * jax.nn.silu(attn_gate)
        x = x + jnp.concatenate([rnn_out, attn_out], axis=-1) @ w_out[l]
    return rms_norm(x, normf_w)
```

```python
import contextlib
import numpy as np
import concourse.bass as bass
import concourse.mybir as mybir
from concourse.bass_utils import run_bass_kernel_spmd

F32 = mybir.dt.float32
BF16 = mybir.dt.bfloat16
AF = mybir.ActivationFunctionType
ALU = mybir.AluOpType
AX = mybir.AxisListType

S = 2048
D = 1024
NG = 4
NCOLW = 2918
TCOL0 = 2662
NEG = -30000.0

C_IDB, C_TRID, C_TRIA, C_OV, C_SELQ, C_BAND, C_CMB0, C_END = 0, 128, 256, 384, 416, 1184, 1344, 1856
F_N1, F_CONVW, F_CONVB, F_BA, F_BX, F_LAM, F_C8, F_B1, F_PATK, F_PATA, F_PEC, F_ONE, F_HBA, F_HBX, F_C4, F_HB1, F_HALF, F_MHALF, F_END = 0, 8, 24, 28, 32, 36, 40, 44, 48, 112, 176, 208, 216, 220, 224, 228, 232, 233, 240


class Prog:
    ENGS = ("pe", "act", "dve", "pool", "sp")

    def __init__(self, nc):
        self.nc = nc
        self.ops = {e: [] for e in self.ENGS}
        self.W = {}
        self.R = {}
        self.dma_cnt = {}

    def _deps_for(self, eng, reads, writes, is_dma):
        deps = {}

        def add(key, val, kind):
            if key[0] == "e" and key[1] == eng and not is_dma and kind != "raw" and eng == "pe":
                return
            if key not in deps or deps[key] < val:
                deps[key] = val

        for r in reads:
            for key, val in self.W.get(r, {}).items():
                add(key, val, "raw")
        for w in writes:
            for key, val in self.W.get(w, {}).items():
                add(key, val, "waw")
            for key, val in self.R.get(w, {}).items():
                add(key, val, "war")
        return deps

    @staticmethod
    def _l(x):
        return [x] if isinstance(x, str) else list(x)

    def op(self, eng, fn, reads=(), writes=()):
        reads, writes = self._l(reads), self._l(writes)
        deps = self._deps_for(eng, reads, writes, False)
        idx = len(self.ops[eng])
        self.ops[eng].append(dict(fn=fn, deps=deps, dma=None))
        key = ("e", eng)
        for r in reads:
            self.R.setdefault(r, {})[key] = idx
        for w in writes:
            self.R[w] = {}
            self.W.setdefault(w, {})[key] = idx
        return idx

    def dma(self, q, stream, fn, reads=(), writes=()):
        reads, writes = self._l(reads), self._l(writes)
        deps = self._deps_for(q, reads, writes, True)
        self.dma_cnt[stream] = self.dma_cnt.get(stream, 0) + 16
        val = self.dma_cnt[stream]
        self.ops[q].append(dict(fn=fn, deps=deps, dma=stream))
        key = ("d", stream)
        for r in reads:
            self.R.setdefault(r, {})[key] = val
        for w in writes:
            self.R[w] = {}
            self.W.setdefault(w, {})[key] = val
        return val

    def barrier(self):
        deps = {}
        for e in self.ENGS:
            for i in range(len(self.ops[e]) - 1, -1, -1):
                o = self.ops[e][i]
                if o["fn"] is not None and o["dma"] is None:
                    deps[("e", e)] = i
                    break
        for s, v in self.dma_cnt.items():
            deps[("d", s)] = v
        for e in self.ENGS:
            d = {k: v for k, v in deps.items() if k != ("e", e)}
            self.ops[e].append(dict(fn=None, deps=d, dma=None))

    def wait_all(self, eng, resources):
        deps = {}
        for r in resources:
            for key, val in self.W.get(r, {}).items():
                if key not in deps or deps[key] < val:
                    deps[key] = val
        self.ops[eng].append(dict(fn=None, deps=deps, dma=None))

    def emit(self):
        nc = self.nc
        flagged = {e: set() for e in self.ENGS}
        for e in self.ENGS:
            for o in self.ops[e]:
                for key, val in o["deps"].items():
                    if key[0] == "e":
                        flagged[key[1]].add(val)
        cnt = {}
        for e in self.ENGS:
            c = 0
            m = {}
            fl = flagged[e]
            for i in range(len(self.ops[e])):
                if i in fl:
                    c += 1
                m[i] = c
            cnt[e] = m
        self.max_counts = {e: (max(cnt[e].values()) if cnt[e] else 0) for e in self.ENGS}
        sems = {}
        with contextlib.ExitStack() as st:
            for e in self.ENGS:
                sems[("e", e)] = st.enter_context(nc.semaphore("s_" + e))
            for s in self.dma_cnt:
                sems[("d", s)] = st.enter_context(nc.semaphore("d_" + s))
            block = st.enter_context(nc.Block())
            engmap = {"pe": block.tensor, "act": block.scalar, "dve": block.vector,
                      "pool": block.gpsimd, "sp": block.sync}
            for e in self.ENGS:
                ops = self.ops[e]
                if not ops:
                    continue
                fl = flagged[e]

                def body(eng, e=e, ops=ops, fl=fl):
                    seen = {}
                    for i, o in enumerate(ops):
                        for key, val in o["deps"].items():
                            v = cnt[key[1]][val] if key[0] == "e" else val
                            if seen.get(key, 0) >= v:
                                continue
                            seen[key] = v
                            eng.wait_ge(sems[key], v)
                        if o["fn"] is None:
                            continue
                        ins = o["fn"](eng)
                        if o["dma"] is not None:
                            ins.then_inc(sems[("d", o["dma"])], 16)
                        elif i in fl:
                            ins.then_inc(sems[("e", e)], 1)

                engmap[e](body)


class Mem:
    def __init__(self, nc):
        self.nc = nc
        self.base = (nc.sbuf_base + 31) // 32 * 32
        self.top = nc.sbuf_top
        self.cur = self.base
        self.hi = self.base

    def alloc(self, name, shape, dt):
        n = 1
        for s in shape[1:]:
            n *= s
        nbytes = (n * (4 if dt == F32 else 2) + 31) // 32 * 32
        t = self.nc.alloc_sbuf_tensor_at(name, list(shape), dt, offset=self.cur)
        self.cur += nbytes
        self.hi = max(self.hi, self.cur)
        assert self.cur <= self.top, f"SBUF overflow at {name}: {self.cur - self.top} bytes over"
        return t


def build_program(nseq=2, dbg=(), stop_after=None):
    nc = bass.Bass("TRN2", target_bir_lowering=False)
    dram = lambda name, shape, kind="ExternalInput": nc.dram_tensor(name, list(shape), F32, kind=kind).ap()
    x_d = dram("x", [nseq, S, D])
    win_d = dram("w_in", [D, NCOLW])
    wout_d = dram("w_out", [D, D])
    wax_d = dram("wax", [128, 4 * 2 * 128])
    w1_d = dram("w1", [128, 32 * 256])
    w2_d = dram("w2", [128, 2 * 2 * 64])
    nf_d = dram("nf", [D])
    cb_d = dram("cb", [128, C_END])
    cf_d = dram("cf", [128, F_END])
    er_d = dram("erows", [32, S])
    out_d = dram("out", [nseq, S, D], kind="ExternalOutput")
    dbg_out = {}

    P = Prog(nc)
    M = Mem(nc)
    st = contextlib.ExitStack()
    PS = [st.enter_context(nc.psum_tensor(f"ps{b}", [128, 512], F32)) for b in range(7)]
    PSB = st.enter_context(nc.psum_tensor("psb", [128, 8, 128], BF16))

    WIN = M.alloc("WIN", [128, 8, NCOLW], BF16)
    WOUT = M.alloc("WOUT", [128, 8, D], BF16)
    W2 = M.alloc("W2", [128, 2, 2, 64], BF16)
    WAX = M.alloc("WAX", [128, 4, 2, 128], BF16)
    CB = M.alloc("CB", [128, C_END], BF16)
    CF = M.alloc("CF", [128, F_END], F32)
    SMALL = M.alloc("SMALL", [128, 80], F32)
    PEB = M.alloc("PEB", [128, 32], BF16)
    HST = M.alloc("HST", [128, 4], F32)
    IMPT = M.alloc("IMPT", [128, 4, 48], F32)
    MBTOK = M.alloc("MBTOK", [128, 4, 32], BF16)
    RG = M.alloc("RG", [128, 4, S], BF16)
    Q = M.alloc("Q", [128, 4, S], BF16)
    AG = M.alloc("AG", [128, 4, S], BF16)
    KE = M.alloc("KE", [96, 2, S], BF16)
    KW = M.alloc("KW", [128, 2, S], BF16)
    EN = M.alloc("EN", [128, S], F32)
    V = M.alloc("V", [128, 16, 4, 66], BF16)
    KC = M.alloc("KC", [128, 2, 128], BF16)
    VC = M.alloc("VC", [128, 2, 66], BF16)
    ov_base = M.cur

    XT = [M.alloc(f"XT{j}", [128, D], F32) for j in range(2)]
    XS = [M.alloc(f"XS{j}", [128, D], BF16) for j in range(2)]
    HT = [M.alloc(f"HT{j}", [128, 8, 512], BF16) for j in range(2)]
    HIDS = nc.alloc_sbuf_tensor_at("HIDS", [128, 2, 2, 2, 128], BF16, offset=M.cur - 2 * 8192 - 2 * 2048)
    W1 = nc.alloc_sbuf_tensor_at("W1", [128, 32, 256], BF16, offset=M.cur - 2 * 8192)
    RXG = [M.alloc(f"RXG{j}", [128, 4, 515], BF16) for j in range(2)]
    XC = M.alloc("XC", [128, 512], F32)
    XCV = [M.alloc(f"XCV{j}", [128, 512], F32) for j in range(2)]
    XCB = M.alloc("XCB", [128, 512], BF16)
    AA = M.alloc("AA", [128, 512], F32)
    UU = M.alloc("UU", [128, 512], F32)
    TH = M.alloc("TH", [128, 512], BF16)
    KVC = M.alloc("KVC", [128, 2, S], BF16)
    p1_end = M.cur
    M.cur = ov_base
    PT = [M.alloc(f"PT{j}", [128, 512], BF16) for j in range(4)]
    QM = [M.alloc(f"QM{j}", [96, 512], BF16) for j in range(4)]
    OSP = [M.alloc(f"OSP{j}", [128, 3, 512], F32) for j in range(2)]
    OSPC = [M.alloc(f"OSPC{j}", [128, 2, 512], F32) for j in range(2)]
    DENSB = [M.alloc("DENSB", [97, 2, 512], F32)] * 2
    DEN = M.alloc("DEN", [128, 512], F32)
    PRD = M.alloc("PRD", [128, 512], F32)
    SHI = M.alloc("SHI", [128, 512], BF16)
    SLO = M.alloc("SLO", [128, 512], BF16)
    MIXA = M.alloc("MIXA", [128, 4, 512], BF16)
    MBT = M.alloc("MBT", [32, 512], BF16)
    XR = [M.alloc(f"XR{j}", [128, D], F32) for j in range(2)]
    WNF = M.alloc("WNF", [128, D], F32)
    JUNK = M.alloc("JUNK", [128, 512], BF16)
    ov_end = max(M.cur, p1_end)
    M.cur = ov_end

    IDB = CB[:, C_IDB:C_IDB + 128]
    TRID = CB[:, C_TRID:C_TRID + 128]
    TRIA = CB[:, C_TRIA:C_TRIA + 128]

    def dump(name, ap, shape, dt, reads, dst=None):
        if name not in dbg:
            return
        if name not in dbg_out:
            dbg_out[name] = nc.dram_tensor("dbg_" + name, list(shape), dt, kind="ExternalOutput").ap()
        t = dbg_out[name]
        d_ap = t if dst is None else dst(t)
        P.dma("sp", "dbg", lambda e: e.dma_start(out=d_ap, in_=ap), reads=reads, writes="dbg_" + name)

    def mm(out, lhsT, rhs, start, stop, reads, writes, sgc=False):
        P.op("pe", lambda e: e.matmul(out, lhsT=lhsT, rhs=rhs, start=start, stop=stop, skip_group_check=sgc), reads, writes)

    def act(out, in_, func, reads, writes, **kw):
        P.op("act", lambda e: e.activation(out=out, in_=in_, func=func, **kw), reads, writes)

    def vcopy(eng, out, in_, reads, writes):
        P.op(eng, lambda e: e.tensor_copy(out=out, in_=in_), reads, writes)

    def tsc(eng, out, in0, s1, s2, op0, op1, reads, writes):
        if op1 is None:
            P.op(eng, lambda e: e.tensor_scalar(out=out, in0=in0, scalar1=s1, scalar2=None, op0=op0), reads, writes)
        else:
            P.op(eng, lambda e: e.tensor_scalar(out=out, in0=in0, scalar1=s1, scalar2=s2, op0=op0, op1=op1), reads, writes)

    def tt(eng, out, in0, in1, op, reads, writes):
        P.op(eng, lambda e: e.tensor_tensor(out=out, in0=in0, in1=in1, op=op), reads, writes)

    def stt(out, in0, scalar, in1, op0, op1, reads, writes):
        P.op("dve", lambda e: e.scalar_tensor_tensor(out=out, in0=in0, scalar=scalar, in1=in1, op0=op0, op1=op1),
             reads, writes)

    def o_tile(p, br):
        if br == 0:
            g2 = (p // 2) % 2
            return OSPC[g2][:, p % 2, :], f"OSPC{g2}:{p % 2}"
        return OSP[p % 2][:, br, :], f"OSP{p % 2}:{br}"

    def evac_o(osb, p, e, br, ob):
        rows = slice(64 * e, 64 * e + 64)
        if br == 0:
            g2 = (p // 2) % 2
            dst, nm = OSPC[g2][rows, p % 2, :], f"OSPC{g2}:{p % 2}:{e}"
        else:
            dst, nm = OSP[osb][rows, br, :], f"OSP{osb}:{br}:{e}"
        act(dst, PS[ob][0:64, :], AF.Copy, reads=f"ps{ob}", writes=nm)
        dsl = br * 2 + e
        dsp = slice(32 * (dsl % 4), 32 * (dsl % 4) + 1)
        if e == 0:
            act(DENSB[0][dsp, dsl // 4, :], PS[ob][64:65, :], AF.Copy, reads=f"ps{ob}", writes=f"DSB:{dsl}")
        else:
            vcopy("dve", DENSB[0][dsp, dsl // 4, :], PS[ob][64:65, :], reads=f"ps{ob}", writes=f"DSB:{dsl}")
        drow = 32 * p + br * 2 + e
        P.dma("sp", f"den{p}_{br}", lambda en: en.dma_start(out=DEN[drow:drow + 1, :], in_=DENSB[0][dsp, dsl // 4, :]),
              reads=f"DSB:{dsl}", writes=f"DEN{p}")

    bank_rr = {"i": 0}

    P.dma("sp", "c_cf", lambda e: e.dma_start(out=CF[:], in_=cf_d), writes="CF")
    P.dma("pool", "c_cb", lambda e: e.dma_start(out=CB[:], in_=cb_d), writes="CB")
    for g in range(2):
        P.dma("pool", "c_er", (lambda g: lambda e: e.dma_start(out=KE[64:96, g, :], in_=er_d))(g), writes=f"KEE{g}")
    P.dma("pool", "c_wax", lambda e: e.dma_start(out=WAX[:].rearrange("p a b c -> p (a b c)"), in_=wax_d), writes="WAX")
    P.dma("pool", "c_w2", lambda e: e.dma_start(out=W2[:].rearrange("p a b c -> p (a b c)"), in_=w2_d), writes="W2")
    P.op("pool", lambda e: e.memset(V[:, :, :, 64:65], 1.0), writes="Vones")
    P.op("pool", lambda e: e.memset(VC[:, :, 64:65], 1.0), writes="VCones")
    STG0 = nc.alloc_sbuf_tensor_at("STGa", [128, 8, 512], F32, offset=ov_base)
    STG1 = nc.alloc_sbuf_tensor_at("STGb", [128, 8, 512], F32, offset=ov_base + 16384)
    assert ov_base + 32768 <= p1_end
    STGs = [STG0, STG1]
    win_v = win_d.rearrange("(k p) n -> p k n", p=128)
    nblk = (NCOLW + 511) // 512
    for blk in range(nblk):
        c0, c1 = blk * 512, min(NCOLW, blk * 512 + 512)
        sg = STGs[blk % 2]
        P.dma("sp", f"stg{blk % 2}", (lambda sg, c0, c1: lambda e: e.dma_start(out=sg[:, :, 0:c1 - c0], in_=win_v[:, :, c0:c1]))(sg, c0, c1),
              writes=f"STG{blk % 2}")
        for k in range(8):
            eng = "dve" if k % 2 == 0 else "pool"
            tsc(eng, WIN[:, k, c0:c1], sg[:, k, 0:c1 - c0], CF[:, F_N1 + k:F_N1 + k + 1], 0.0, ALU.mult, ALU.add,
                reads=[f"STG{blk % 2}", "CF"], writes=f"WIN{blk}")
    P.dma("pool", "c_wout", lambda e: e.dma_start(out=WOUT[:], in_=wout_d.rearrange("(k p) n -> p k n", p=128)), writes="WOUT")
    EE = SMALL[:, 32:36]
    TT = SMALL[:, 36:40]
    act(EE, CF[:, F_LAM:F_LAM + 4], AF.Exp, reads="CF", writes="EE", scale=-1.0)
    tsc("dve", TT, EE, -0.25, 1.0 / 3.0, ALU.mult, ALU.add, reads="EE", writes="TTs")
    tt("dve", TT, TT, EE, ALU.mult, reads=["TTs", "EE"], writes="TTs")
    tsc("dve", TT, TT, -1.0, 0.5, ALU.mult, ALU.add, reads="TTs", writes="TTs")
    tt("dve", TT, TT, EE, ALU.mult, reads=["TTs", "EE"], writes="TTs")
    tsc("dve", TT, TT, -1.0, 1.0, ALU.mult, ALU.add, reads="TTs", writes="TTs")
    tt("dve", TT, TT, EE, ALU.mult, reads=["TTs", "EE"], writes="TTs")
    tsc("dve", CF[:, F_C8:F_C8 + 4], TT, -8.0, None, ALU.mult, None, reads="TTs", writes="C8")
    tsc("dve", CF[:, F_C4:F_C4 + 4], CF[:, F_C8:F_C8 + 4], 0.5, None, ALU.mult, None, reads="C8", writes="C4")
    tsc("dve", CF[:, F_HBA:F_HBA + 4], CF[:, F_BA:F_BA + 4], 0.5, None, ALU.mult, None, reads="CF", writes="HBA")
    tsc("dve", CF[:, F_HBX:F_HBX + 4], CF[:, F_BX:F_BX + 4], 0.5, None, ALU.mult, None, reads="CF", writes="HBX")
    tsc("dve", WOUT[:].rearrange("p a b -> p (a b)"), WOUT[:].rearrange("p a b -> p (a b)"), 0.5, None, ALU.mult, None, reads="WOUT", writes="WOUT")
    tsc("dve", W2[:].rearrange("p a b c -> p (a b c)"), W2[:].rearrange("p a b c -> p (a b c)"), 0.5, None, ALU.mult, None, reads="W2", writes="W2")
    P.barrier()

    x_t = x_d.rearrange("s (i p) d -> s i p d", p=128)
    out_t = out_d.rearrange("s (i p) d -> s i p d", p=128)

    def chunk_cols(c):
        if c < 20:
            return c * 128, 128
        return 2560, 102

    for s in range(nseq):
        def stage_front(i, s=s):
            j = i % 2
            P.dma("sp", f"xt{j}", (lambda j, i, s: lambda e: e.dma_start(out=XT[j][:], in_=x_t[s, i]))(j, i, s), writes=f"XT{j}")
            ssq = SMALL[:, i:i + 1]
            rs = SMALL[:, 16 + i:16 + i + 1]
            act(XS[j][:], XT[j][:], AF.Square, reads=f"XT{j}", writes=[f"XS{j}", f"ssq{i}"], accum_out=ssq)
            tsc("dve", rs, ssq, 1.0 / D, 1e-6, ALU.mult, ALU.add, reads=f"ssq{i}", writes=f"rs{i}")
            tt("pool", rs, rs, CF[:, F_MHALF:F_MHALF + 1], ALU.pow, reads=[f"rs{i}", "CF"], writes=f"rs{i}")
            tsc("dve", XS[j][:], XT[j][:], rs, None, ALU.mult, None, reads=[f"XT{j}", f"rs{i}"], writes=f"XS{j}")

        def stage_back(i):
            n_, i4 = i // 4, i % 4
            hs = n_ % 2
            j = i % 2
            for k in range(8):
                P.op("pe", (lambda j, k: lambda e: e.transpose(out=PSB[:, k, :], in_=XS[j][:, k * 128:(k + 1) * 128], identity=IDB))(j, k),
                     reads=[f"XS{j}", "CB"], writes="psb")
            act(HT[hs][:, :, i4 * 128:(i4 + 1) * 128], PSB[:], AF.Copy, reads="psb", writes=f"HT{hs}:{i4}")

        def proj_chunk(n, c):
            hs = n % 2
            rb = n % 2
            c_tok = slice(n * 512, (n + 1) * 512)
            c0, cw = chunk_cols(c)
            b = bank_rr["i"] % 6
            bank_rr["i"] += 1
            pb = PS[b]
            htr = [f"HT{hs}:{q}" for q in range(4)]
            for k in range(8):
                mm(pb[0:cw, :], WIN[:, k, c0:c0 + cw], HT[hs][:, k, :], k == 0, k == 7,
                   reads=[f"WIN{c0 // 512}"] + htr, writes=f"ps{b}")
            rd = f"ps{b}"
            if c < 4:
                if c % 2 == 0:
                    act(RXG[rb][:, c, 3:515], pb[:], AF.Copy, reads=rd, writes=f"RX{rb}:{c}")
                else:
                    vcopy("dve", RXG[rb][:, c, 3:515], pb[:], reads=rd, writes=f"RX{rb}:{c}")
            elif c < 8:
                act(TH[:], pb[:], AF.Tanh, reads=rd, writes="TH", scale=0.5)
                stt(RG[:, c - 4, c_tok], TH[:], 1.0, pb[:], ALU.add, ALU.mult, reads=["TH", rd], writes=f"RG{c - 4}:{n}")
            elif c < 12:
                tsc("dve", Q[:, c - 8, c_tok], pb[:], 0.125, None, ALU.mult, None, reads=rd, writes=[f"Q{c - 8}:{n}"])
            elif c < 14:
                vcopy("dve", KVC[:, c - 12, c_tok], pb[:], reads=rd, writes=f"KVC{c - 12}:{n}")
            elif c == 14:
                vcopy("dve", KE[0:64, 0, c_tok], pb[0:64, :], reads=rd, writes=f"KE0:{n}")
                act(KE[0:64, 1, c_tok], pb[64:128, :], AF.Copy, reads=rd, writes=f"KE1:{n}")
            elif c == 15:
                vcopy("dve", KW[0:64, 0, c_tok], pb[0:64, :], reads=rd, writes=f"KW0:{n}")
                act(KW[64:128, 0, c_tok], pb[0:64, :], AF.Copy, reads=rd, writes=f"KW0:{n}")
                vcopy("dve", KW[64:128, 1, c_tok], pb[64:128, :], reads=rd, writes=f"KW1:{n}")
                act(KW[0:64, 1, c_tok], pb[64:128, :], AF.Copy, reads=rd, writes=f"KW1:{n}")
            elif c < 20:
                act(TH[:], pb[:], AF.Tanh, reads=rd, writes="TH", scale=0.5)
                stt(AG[:, c - 16, c_tok], TH[:], 1.0, pb[:], ALU.add, ALU.mult, reads=["TH", rd], writes=[f"AG{c - 16}:{n}"])
            else:
                act(EN[0:102, c_tok], pb[0:102, :], AF.Exp, reads=rd, writes=f"EN:{n}", scale=-1.0)

        def v_part(n):
            hs = n % 2
            for i4 in range(4):
                i = 4 * n + i4
                b = bank_rr["i"] % 6
                bank_rr["i"] += 1
                pb = PS[b]
                for k in range(8):
                    mm(pb[:, 0:256], HT[hs][:, k, i4 * 128:(i4 + 1) * 128], WIN[:, k, TCOL0:TCOL0 + 256], k == 0, k == 7,
                       reads=["WIN5", f"HT{hs}:{i4}"], writes=f"ps{b}")
                vcopy("dve", V[:, i, :, 0:64], pb[:, 0:256].rearrange("p (a b) -> p a b", a=4), reads=f"ps{b}", writes=f"V{i}")

        def halo(n):
            rb = n % 2
            if n == 0:
                P.op("pool", lambda e: e.memset(RXG[0][:, :, 0:3], 0.0), writes=[f"RXh{rb}"])
            else:
                vcopy("pool", RXG[rb][:, :, 0:3], RXG[1 - rb][:, :, 512:515], reads=[f"RX{1 - rb}:{c}" for c in range(4)], writes=f"RXh{rb}")

        def lru_conv(n, c):
            rb = n % 2
            xv = XCV[c % 2]
            rxr = [f"RX{rb}:{c}", f"RXh{rb}"]
            cw = lambda k: CF[:, F_CONVW + 4 * c + k:F_CONVW + 4 * c + k + 1]
            tsc("dve", xv[:], RXG[rb][:, c, 3:515], cw(3), CF[:, F_CONVB + c:F_CONVB + c + 1], ALU.mult, ALU.add,
                reads=rxr + ["CF"], writes=f"XCV{c % 2}")
            for k in (2, 1, 0):
                stt(xv[:], RXG[rb][:, c, k:512 + k], cw(k), xv[:], ALU.mult, ALU.add, reads=rxr + ["CF", f"XCV{c % 2}"], writes=f"XCV{c % 2}")

        def lru_cast(n, c):
            vcopy("pool", XCB[:], XCV[c % 2][:], reads=f"XCV{c % 2}", writes="XCB")

        def lru_a(n, c):
            lru_conv(n, c)
            lru_cast(n, c)

        def lru_b(n, c, s=s):
            c_tok = slice(n * 512, (n + 1) * 512)
            for ai, (dst, boff, nm) in enumerate(((AA, F_HBA, "AA"), (UU, F_HBX, "UU"))):
                b = 6
                mm(PS[b][:, :], WAX[:, c, ai, :], XCB[:], True, True, reads=["WAX", "XCB"], writes=f"ps{b}")
                act(dst[:], PS[b][:, :], AF.Tanh, reads=[f"ps{b}", "HBA", "HBX"], writes=nm, scale=0.5, bias=CF[:, boff + c:boff + c + 1])
            act(AA[:], AA[:], AF.Exp, reads=["AA", "C4"], writes="AA", scale=CF[:, F_C4 + c:F_C4 + c + 1], bias=CF[:, F_C4 + c:F_C4 + c + 1])
            stt(UU[:], UU[:], 1.0, XCV[c % 2][:], ALU.add, ALU.mult, reads=["UU", f"XCV{c % 2}"], writes="UU")
            tt("pool", XC[:], AA[:], AA[:], ALU.mult, reads=["AA"], writes="XC")
            tsc("pool", XC[:], XC[:], -1.0, 1.0, ALU.mult, ALU.add, reads="XC", writes="XC")
            act(XC[:], XC[:], AF.Sqrt, reads="XC", writes="XC")
            stt(UU[:], UU[:], 0.5, XC[:], ALU.mult, ALU.mult, reads=["UU", "XC"], writes="UU")
            init = 0.0 if n == 0 else HST[:, c:c + 1]
            P.op("dve", (lambda init: lambda e: e.tensor_tensor_scan(out=XC[:], data0=AA[:], data1=UU[:], initial=init, op0=ALU.mult, op1=ALU.add))(init),
                 reads=["AA", "UU", f"HST{c}"], writes="XC")
            vcopy("dve", HST[:, c:c + 1], XC[:, 511:512], reads="XC", writes=f"HST{c}")
            if s == 0 and c == 0:
                dump("LRU0", XC[:], [128, S], F32, ["XC"], dst=(lambda c_tok: lambda t: t[:, c_tok])(c_tok))
            tt("dve", RG[:, c, c_tok], XC[:], RG[:, c, c_tok], ALU.mult, reads=["XC", f"RG{c}:{n}"], writes=f"RG{c}:{n}")

        stage_front(0)
        for i in range(4):
            stage_front(i + 1)
            stage_back(i)
        for n in range(NG):
            halo(n)
            for c in range(21):
                proj_chunk(n, c)
                if n > 0:
                    if c == 0:
                        lru_a(n - 1, 0)
                    if c in (3, 8, 13, 18):
                        t_ = (c - 3) // 5
                        if t_ < 3:
                            lru_conv(n - 1, t_ + 1)
                        lru_b(n - 1, t_)
                        if t_ < 3:
                            lru_cast(n - 1, t_ + 1)
                if n < NG - 1 and c in (3, 8, 13, 18):
                    ti = 4 * (n + 1) + (c - 3) // 5
                    stage_back(ti)
                    if ti + 1 < 16:
                        stage_front(ti + 1)
            v_part(n)
        P.dma("pool", "c_w1", lambda e: e.dma_start(out=W1[:].rearrange("p a b -> p (a b)"), in_=w1_d),
              writes=["W1"] + [f"HT{a}:{q}" for a in range(2) for q in range(4)])
        if s == 0:
            vcopy("dve", PEB[:], CF[:, F_PEC:F_PEC + 32], reads="CF", writes="PEB")
            for kv in range(2):
                rows = slice(64 * kv, 64 * kv + 64)
                b = 2 * kv
                for hc in range(2):
                    col = kv * 2 + hc
                    for l in range(32):
                        mm(PS[b][:, col:col + 1], W1[rows, l, hc * 128:(hc + 1) * 128], PEB[rows, l:l + 1], l == 0, l == 31,
                           reads=["W1", "PEB"], writes=f"ps{b}")
                vcopy("dve", CF[:, F_B1 + 2 * kv:F_B1 + 2 * kv + 2], PS[b][:, 2 * kv:2 * kv + 2], reads=f"ps{b}", writes="B1")
                tsc("dve", CF[:, F_HB1 + 2 * kv:F_HB1 + 2 * kv + 2], PS[b][:, 2 * kv:2 * kv + 2], 0.5, None, ALU.mult, None, reads=f"ps{b}", writes="HB1")
        kvc_all = [f"KVC{g}:{n}" for g in range(2) for n in range(NG)]
        blk_i = 0
        for kv in range(2):
            rows = slice(64 * kv, 64 * kv + 64)
            for g in range(2):
                for hc in range(2):
                    b = 2 * kv + hc
                    for l in range(32):
                        mm(PS[b][:, 0:127], W1[rows, l, hc * 128:(hc + 1) * 128], KVC[rows, g, l:l + 2017:16], l == 0, l == 31,
                           reads=["W1"] + kvc_all, writes=f"ps{b}")
                    bcol = kv * 2 + hc
                    act(TH[:, 0:127], PS[b][:, 0:127], AF.Tanh, reads=[f"ps{b}", "HB1"], writes="TH", scale=0.5,
                        bias=CF[:, F_HB1 + bcol:F_HB1 + bcol + 1])
                    tsc("dve", TH[:, 0:127], TH[:, 0:127], 1.0, None, ALU.add, None, reads="TH", writes="TH")
                    stt(HIDS[:, kv, g, hc, 0:127], PS[b][:, 0:127], CF[:, F_B1 + bcol:F_B1 + bcol + 1], TH[:, 0:127], ALU.add, ALU.mult,
                        reads=[f"ps{b}", "B1", "TH"], writes=[f"HIDS{kv}{g}{hc}", "XS0"])
                    if blk_i % 2 == 0:
                        lru_a(NG - 1, blk_i // 2)
                    else:
                        lru_b(NG - 1, blk_i // 2)
                    blk_i += 1
        for g in range(2):
            b = 0
            for hc in range(2):
                mm(PS[b][0:64, 0:127], W2[:, 0, hc, :], HIDS[:, 0, g, hc, 0:127], hc == 0, hc == 1,
                   reads=["W2", f"HIDS0{g}{hc}"], writes=f"ps{b}")
            vcopy("dve", KC[0:64, g, 0:127], PS[b][0:64, 0:127], reads=f"ps{b}", writes=f"KC{g}")
            act(KC[64:128, g, 0:127], PS[b][0:64, 0:127], AF.Copy, reads=f"ps{b}", writes=f"KC{g}")
            b = 1
            for hc in range(2):
                mm(PS[b][0:127, 0:64], HIDS[:, 1, g, hc, 0:127], W2[:, 1, hc, :], hc == 0, hc == 1,
                   reads=["W2", f"HIDS1{g}{hc}"], writes=f"ps{b}")
            vcopy("dve", VC[0:127, g, 0:64], PS[b][0:127, 0:64], reads=f"ps{b}", writes=f"VC{g}")
        P.barrier()
        if s == 0:
            allq = lambda nm, cnt: [f"{nm}{c}:{n}" for c in range(cnt) for n in range(NG)]
            dump("KC", KC[:], [128, 2, 128], BF16, ["KC0", "KC1"])
            dump("VC", VC[:], [128, 2, 66], BF16, ["VC0", "VC1", "VCones"])
            dump("RNN", RG[:], [128, 4, S], BF16, allq("RG", 4))
            dump("Q", Q[:], [128, 4, S], BF16, allq("Q", 4))
            dump("AG", AG[:], [128, 4, S], BF16, allq("AG", 4))
            dump("WOUT", WOUT[:], [128, 8, D], BF16, ["WOUT"])
            dump("V", V[:], [128, 16, 4, 66], BF16, [f"V{i}" for i in range(16)])
        if stop_after == 2:
            break
        P.op("pool", lambda e: e.memset(DEN[:], 1.0), writes=[f"DEN{p}" for p in range(4)])
        P.op("pool", lambda e: e.memset(SHI[:], 0.0), writes=["SHI"] + [f"SHI{p}" for p in range(4)])
        P.op("pool", lambda e: e.memset(SLO[:], 0.0), writes=["SLO"] + [f"SLO{p}" for p in range(4)])
        P.op("pool", lambda e: e.memset(PRD[:], 1.0), writes=[f"PRD{p}" for p in range(4)])
        P.dma("sp", "c_nf", lambda e: e.dma_start(out=WNF[:], in_=nf_d.partition_broadcast(128)), writes="WNF")
        rr = {"s": 0, "o": 0, "pt": 0, "qm": 0}

        def nxt(key, mod):
            v = rr[key] % mod
            rr[key] += 1
            return v

        prev_fns = None
        for n in range(NG):
            T0 = n * 512
            tk = slice(T0, T0 + 512)
            ncv = min(127, 32 * n + 31)
            bs0 = 128 - 32 * n

            def cmp_item(g, r4, n=n, T0=T0, tk=tk, ncv=ncv, bs0=bs0):
                h = 4 * g + r4
                p, e = h // 2, h % 2
                rows = slice(64 * e, 64 * e + 64)
                st_ = {}

                def qk():
                    sb = st_["sb"] = nxt("s", 3)
                    mm(PS[sb][0:ncv, :], KC[rows, g, 0:ncv], Q[rows, p, tk], True, False,
                       reads=[f"KC{g}", f"Q{p}:{n}"], writes=f"ps{sb}")
                    mm(PS[sb][0:ncv, :], CB[rows, C_BAND + bs0:C_BAND + bs0 + ncv], CB[rows, C_CMB0:C_CMB0 + 512], False, True,
                       reads=["CB"], writes=f"ps{sb}")
                    pt = st_["pt"] = nxt("pt", 4)
                    act(PT[pt][0:ncv, :], PS[sb][0:ncv, :], AF.Exp, reads=f"ps{sb}", writes=f"PT{pt}")

                def pv():
                    pt = st_["pt"]
                    ob = 3 + nxt("o", 2)
                    mm(PS[ob][0:65, :], VC[0:ncv, g, 0:65], PT[pt][0:ncv, :], True, True,
                       reads=[f"VC{g}", "VCones", f"PT{pt}"], writes=f"ps{ob}")
                    evac_o(p % 2, p, e, 0, ob)
                    for j in range(4):
                        mm(PS[6][:, (j * 4 + r4) * 32:(j * 4 + r4) * 32 + 32], PT[pt][0:ncv, j * 128:(j + 1) * 128], CB[0:ncv, C_OV:C_OV + 32],
                           True, True, reads=[f"PT{pt}", "CB"], writes="ps6")
                return qk, pv

            def imp_dve(g, n=n):
                ib = 6
                psv = PS[ib][:, :]
                rd = SMALL[:, 52:68]
                P.op("dve", lambda en: en.tensor_reduce(out=rd, in_=psv.rearrange("p (a b) -> p a b", a=16), axis=AX.X, op=ALU.add),
                     reads=f"ps{ib}", writes="rd")
                tsc("dve", rd, rd, 1e-30, None, ALU.max, None, reads="rd", writes="rd")
                P.op("dve", lambda en: en.reciprocal(out=rd, in_=rd), reads="rd", writes="rd")
                for j in range(4):
                    i = 4 * n + j
                    x0 = 32 - 2 * i
                    blkv = PS[ib][:, j * 128:(j + 1) * 128]
                    im = IMPT[:, j, 0:32]
                    tsc("dve", im, blkv[:, 0:32], rd[:, 4 * j:4 * j + 1], None, ALU.mult, None, reads=[f"ps{ib}", "rd"], writes=f"im{j}")
                    for r4 in range(1, 4):
                        stt(im, blkv[:, 32 * r4:32 * r4 + 32], rd[:, 4 * j + r4:4 * j + r4 + 1], im, ALU.mult, ALU.add,
                            reads=[f"ps{ib}", "rd", f"im{j}"], writes=f"im{j}")
                    if s == 0:
                        dump("IMP", im, [128, 16, 2, 32], F32, [f"im{j}"], dst=(lambda i, g: lambda t: t[:, i, g, :])(i, g))
                    tt("dve", im, im, CF[:, F_PATK + x0:F_PATK + x0 + 32], ALU.mult, reads=[f"im{j}", "CF"], writes=f"im{j}")
                    tt("dve", im, im, CF[:, F_PATA + x0:F_PATA + x0 + 32], ALU.add, reads=[f"im{j}", "CF"], writes=f"im{j}")
                    P.op("dve", (lambda j: lambda en: en.memset(IMPT[:, j, 0:1], 1e9))(j), reads=f"im{j}", writes=f"im{j}")
                    t8 = IMPT[:, j, 32:40]
                    P.op("dve", (lambda t8, im: lambda en: en.max(out=t8, in_=im))(t8, im), reads=f"im{j}", writes=f"t8{j}")
                    thr = IMPT[:, j, 40:41]
                    tsc("dve", thr, t8[:, 7:8], -5e8, None, ALU.max, None, reads=f"t8{j}", writes=f"thr{j}")
                    tsc("dve", MBTOK[:, j, :], im, thr, NEG, ALU.is_lt, ALU.mult, reads=[f"im{j}", f"thr{j}"], writes=f"MBTOK{j}")

            def mbt_pe(g, n=n, tk=tk):
                for j in range(4):
                    P.op("pe", (lambda j: lambda en: en.transpose(out=PSB[0:32, j, :], in_=MBTOK[:, j, :], identity=IDB))(j),
                         reads=[f"MBTOK{j}", "CB"], writes="psb")
                vcopy("dve", MBT[0:32, :], PSB[0:32, 0:4, :], reads="psb", writes="MBT")
                for p_ in (2 * g, 2 * g + 1):
                    for e_ in range(2):
                        qm = 2 * (p_ % 2) + e_
                        vcopy("pool" if e_ == 0 else "dve", QM[qm][0:64, :], Q[64 * e_:64 * e_ + 64, p_, tk], reads=f"Q{p_}:{n}", writes=f"QM{qm}")
                        vcopy("dve", QM[qm][64:96, :], MBT[0:32, :], reads="MBT", writes=f"QM{qm}")

            def sel_items(p, e, n=n, T0=T0, tk=tk):
                g = p // 2
                rows = slice(64 * e, 64 * e + 64)
                nkc = 4 * n + 4
                st_ = {}
                items = []
                for kc in range(nkc):
                    c0 = max(0, 128 * kc - T0)
                    diag = kc >= 4 * n
                    it = {}

                    def qk(kc=kc, c0=c0, diag=diag, it=it):
                        qm = 2 * (p % 2) + e
                        sb = nxt("s", 3)
                        mm(PS[sb][:, c0:512], KE[0:96, g, kc * 128:(kc + 1) * 128], QM[qm][0:96, c0:512], True, not diag,
                           reads=[f"KE{g}:{kc // 4}", f"KEE{g}", f"QM{qm}"], writes=f"ps{sb}")
                        if diag:
                            mm(PS[sb][:, c0:c0 + 128], IDB, TRID, False, True, reads=["CB"], writes=f"ps{sb}")
                        pt = it["pt"] = nxt("pt", 4)
                        act(PT[pt][:, c0:512], PS[sb][:, c0:512], AF.Exp, reads=f"ps{sb}", writes=f"PT{pt}")

                    def pv(kc=kc, c0=c0, it=it):
                        if kc == 0:
                            st_["ob"] = 3 + nxt("o", 2)
                        ob = st_["ob"]
                        pt = it["pt"]
                        mm(PS[ob][0:65, c0:512], V[:, kc, g, 0:65], PT[pt][:, c0:512], kc == 0, kc == nkc - 1,
                           reads=[f"V{kc}", "Vones", f"PT{pt}"], writes=f"ps{ob}")
                        if kc == nkc - 1:
                            evac_o(p % 2, p, e, 1, ob)
                    items.append((qk, pv))
                return items

            def win_items(p, e, n=n, T0=T0):
                g = p // 2
                rows = slice(64 * e, 64 * e + 64)
                kcs = list(range(max(0, 4 * n - 4), 4 * n + 4))
                st_ = {}
                items = []
                for kc in kcs:
                    i_lo, i_hi = max(kc, 4 * n), min(kc + 4, 4 * n + 3)
                    c0, c1 = (i_lo - 4 * n) * 128, (i_hi - 4 * n + 1) * 128
                    diag = kc >= 4 * n
                    anti = (kc + 4 >= 4 * n) and (kc + 4 <= 4 * n + 3)
                    it = {}

                    def qk(kc=kc, c0=c0, c1=c1, diag=diag, anti=anti, it=it):
                        sb = nxt("s", 3)
                        mm(PS[sb][:, c0:c1], KW[rows, g, kc * 128:(kc + 1) * 128], Q[rows, p, T0 + c0:T0 + c1], True, not (diag or anti),
                           reads=[f"KW{g}:{kc // 4}", f"Q{p}:{n}"], writes=f"ps{sb}")
                        if diag:
                            mm(PS[sb][:, c0:c0 + 128], IDB, TRID, False, True, reads=["CB"], writes=f"ps{sb}")
                        if anti:
                            mm(PS[sb][:, c1 - 128:c1], IDB, TRIA, False, True, reads=["CB"], writes=f"ps{sb}")
                        pt = it["pt"] = nxt("pt", 4)
                        act(PT[pt][:, c0:c1], PS[sb][:, c0:c1], AF.Exp, reads=f"ps{sb}", writes=f"PT{pt}")

                    def pv(kc=kc, c0=c0, c1=c1, it=it):
                        if kc == kcs[0]:
                            st_["ob"] = 3 + nxt("o", 2)
                        ob = st_["ob"]
                        pt = it["pt"]
                        mm(PS[ob][0:65, c0:c1], V[:, kc, 2 + g, 0:65], PT[pt][:, c0:c1], kc == kcs[0], kc == kcs[-1],
                           reads=[f"V{kc}", "Vones", f"PT{pt}"], writes=f"ps{ob}", sgc=True)
                        if kc == kcs[-1]:
                            evac_o(p % 2, p, e, 2, ob)
                    items.append((qk, pv))
                return items

            def run_stream(items, look=2):
                pend = []
                for it in items:
                    if len(it) > 2:
                        for q_ in pend:
                            q_[1]()
                        pend = []
                    if it[0] is not None:
                        it[0]()
                    if it[1] is not None:
                        pend.append(it)
                    while len(pend) > look:
                        pend.pop(0)[1]()
                for q_ in pend:
                    q_[1]()

            def combine_front(p, n=n, T0=T0, tk=tk):
                win = slice(32 * p, 32 * p + 6)
                tsc("pool", DEN[win, :], DEN[win, :], 3e38, 1e-18, ALU.min, ALU.max, reads=f"DEN{p}", writes=f"DEN{p}")
                stt(PRD[win, :], EN[win, tk], 1.0, DEN[win, :], ALU.add, ALU.mult, reads=[f"EN:{n}", f"DEN{p}"], writes=f"PRD{p}")
                P.op("dve", (lambda win: lambda en: en.reciprocal(out=PRD[win, :], in_=PRD[win, :]))(win), reads=[f"PRD{p}"], writes=[f"PRD{p}"])
                vcopy("dve", SHI[win, :], PRD[win, :], reads=f"PRD{p}", writes=f"SHI{p}")
                tt("dve", SLO[win, :], PRD[win, :], SHI[win, :], ALU.subtract, reads=[f"PRD{p}", f"SHI{p}"], writes=f"SLO{p}")
                if s == 0 and n == 0:
                    dump("SG", PRD[win, :], [128, 512], F32, [f"PRD{p}"], dst=(lambda win: lambda t: t[win, :])(win))

            def combine(p, n=n, T0=T0, tk=tk):
                osb = p % 2
                win = slice(32 * p, 32 * p + 6)
                if p < 3:
                    kw_, sset = win, 0
                else:
                    kw_, sset = slice(64, 102), 1
                for br in range(3):
                    bb = 5
                    ot, onm = o_tile(p, br)
                    ors = [onm + ":0", onm + ":1"]
                    sel_ap = CB[kw_, C_SELQ + (sset * 3 + br) * 128:C_SELQ + (sset * 3 + br) * 128 + 128]
                    mm(PS[bb][:, :], sel_ap, SHI[kw_, :], True, False, reads=["CB", f"SHI{p}", "SHI"], writes=f"ps{bb}")
                    mm(PS[bb][:, :], sel_ap, SLO[kw_, :], False, True, reads=["CB", f"SLO{p}", "SLO"], writes=f"ps{bb}")
                    tt("dve", ot, ot, PS[bb][:, :], ALU.mult, reads=ors + [f"ps{bb}"], writes=ors)
                    if s == 0:
                        dump("OBR", ot, [128, 3, 4, S], F32, ors, dst=(lambda br, p, tk: lambda t: t[:, br, p, tk])(br, p, tk))
                t0, n0 = o_tile(p, 0)
                t1, n1 = o_tile(p, 1)
                t2, n2 = o_tile(p, 2)
                r0, r1, r2 = [n0 + ":0", n0 + ":1"], [n1 + ":0", n1 + ":1"], [n2 + ":0", n2 + ":1"]
                tt("pool", t1, t1, t2, ALU.add, reads=r1 + r2, writes=r1)
                tt("pool", t1, t1, t0, ALU.add, reads=r1 + r0, writes=r1)
                if s == 0:
                    dump("ATT", t1, [128, 4, S], F32, r1, dst=(lambda p, tk: lambda t: t[:, p, tk])(p, tk))
                tt("pool", MIXA[:, p, :], t1, AG[:, p, tk], ALU.mult, reads=r1 + [f"AG{p}:{n}"], writes=f"MIXA{p}")
                if s == 0:
                    dump("MIXA", MIXA[:, p, :], [128, 4, S], BF16, [f"MIXA{p}"], dst=(lambda p, tk: lambda t: t[:, p, tk])(p, tk))

            def xr_load(j, n=n):
                i = 4 * n + j
                xb = i % 2
                P.dma("sp", f"xr{xb}", (lambda xb, i, s: lambda en: en.dma_start(out=XR[xb][:], in_=x_t[s, i]))(xb, i, s), writes=f"XR{xb}")

            def outproj_tile(j, n=n, T0=T0, xr_load_=xr_load):
                i = 4 * n + j
                xb = i % 2
                if s == 0:
                    dump("XRIN", XR[xb][:], [16, 128, D], F32, [f"XR{xb}"], dst=(lambda i: lambda t: t[i])(i))
                for half in range(2):
                    yb = 6
                    for k in range(8):
                        if k < 4:
                            lhs = RG[:, k, T0 + j * 128:T0 + (j + 1) * 128]
                            rdn = [f"RG{k}:{n}"]
                        else:
                            lhs = MIXA[:, k - 4, j * 128:(j + 1) * 128]
                            rdn = [f"MIXA{k - 4}"]
                        mm(PS[yb][:, :], lhs, WOUT[:, k, half * 512:(half + 1) * 512], k == 0, k == 7,
                           reads=rdn + ["WOUT"], writes=f"ps{yb}")
                    tt("dve", XR[xb][:, half * 512:(half + 1) * 512], XR[xb][:, half * 512:(half + 1) * 512], PS[yb][:, :], ALU.add,
                       reads=[f"XR{xb}", f"ps{yb}"], writes=f"XR{xb}")
                ssq = SMALL[:, 44 + xb:45 + xb]
                rs = SMALL[:, 46 + xb:47 + xb]
                if s == 0:
                    dump("Y", XR[xb][:], [16, 128, D], F32, [f"XR{xb}"], dst=(lambda i: lambda t: t[i])(i))
                ssq2 = SMALL[:, 48 + xb:49 + xb]
                act(JUNK[:], XR[xb][:, 0:512], AF.Square, reads=f"XR{xb}", writes=["JUNK", f"fssq{xb}"], accum_out=ssq)
                act(JUNK[:], XR[xb][:, 512:1024], AF.Square, reads=f"XR{xb}", writes=["JUNK", f"fssqb{xb}"], accum_out=ssq2)
                tt("dve", ssq, ssq, ssq2, ALU.add, reads=[f"fssq{xb}", f"fssqb{xb}"], writes=f"fssq{xb}")
                tsc("dve", rs, ssq, 1.0 / D, 1e-6, ALU.mult, ALU.add, reads=f"fssq{xb}", writes=f"frs{xb}")
                tt("pool", rs, rs, CF[:, F_MHALF:F_MHALF + 1], ALU.pow, reads=[f"frs{xb}", "CF"], writes=f"frs{xb}")
                if s == 0:
                    dump("RSF", SMALL[:, 44:48], [16, 128, 4], F32, [f"frs{xb}", f"fssq{xb}"], dst=(lambda i: lambda t: t[i])(i))
                stt(XR[xb][:], XR[xb][:], rs, WNF[:], ALU.mult, ALU.mult, reads=[f"XR{xb}", f"frs{xb}", "WNF"], writes=f"XR{xb}")
                P.dma("sp", f"st{xb}", (lambda xb, i, s: lambda en: en.dma_start(out=out_t[s, i], in_=XR[xb][:]))(xb, i, s), reads=f"XR{xb}", writes="OUT")
                if j < 2:
                    xr_load_(j + 2)

            def special(fn):
                return (fn, None)

            def with_front(sel_list, nxt_list, p):
                k, real = 0, 0
                while k < len(nxt_list) and real < 3:
                    if nxt_list[k][1] is not None:
                        real += 1
                    k += 1
                return sel_list + nxt_list[:k] + [special((lambda p=p: combine_front(p)))] + nxt_list[k:]

            def real_split(lst, k):
                i_, real = 0, 0
                while i_ < len(lst) and real < k:
                    if lst[i_][1] is not None:
                        real += 1
                    i_ += 1
                return i_

            items = []
            has_prev = prev_fns is not None
            items += [cmp_item(0, r4) for r4 in range(4)]
            items.append(((lambda: imp_dve(0)), None, True))
            w0 = win_items(0, 0) + win_items(0, 1)
            pre = [special(prev_fns["front3"]), special(prev_fns["xr01"])] if has_prev else []
            items += w0[:3] + pre + w0[3:]
            w1a = win_items(1, 0)
            w1 = w1a + win_items(1, 1)
            k6 = real_split(w1, min(6, len(w1a) - 1))
            items += w1[:k6] + ([special(prev_fns["combine3"])] if has_prev else []) + w1[k6:]
            items.append(special(lambda: mbt_pe(0)))
            items += sel_items(0, 0) + sel_items(0, 1)
            sel1 = sel_items(1, 0) + sel_items(1, 1)
            outp = [special((lambda f, j: lambda: f(j))(prev_fns["outproj"], j)) for j in range(4)] if has_prev else []
            items += sel1[:3] + [special(lambda: combine_front(0))] + outp + sel1[3:]
            items.append(special(lambda: combine(0)))
            c1 = [cmp_item(1, r4) for r4 in range(4)]
            items += c1[:3] + [special(lambda: combine_front(1))] + c1[3:]
            items.append(((lambda: imp_dve(1)), None, True))
            items += win_items(2, 0) + win_items(2, 1)
            w3a = win_items(3, 0)
            w3 = w3a + win_items(3, 1)
            k6 = real_split(w3, min(6, len(w3a) - 1))
            items += w3[:k6] + [special(lambda: combine(1))] + w3[k6:]
            items.append(special(lambda: mbt_pe(1)))
            items += sel_items(2, 0) + sel_items(2, 1)
            sel3 = sel_items(3, 0) + sel_items(3, 1)
            items += sel3[:3] + [special(lambda: combine_front(2))] + sel3[3:]
            items.append(special(lambda: combine(2)))
            run_stream(items)
            prev_fns = {"front3": (lambda c=combine_front: c(3)), "combine3": (lambda c=combine: c(3)), "outproj": outproj_tile,
                        "xr01": (lambda f=xr_load: (f(0), f(1)))}
        prev_fns["front3"]()
        prev_fns["xr01"]()
        prev_fns["combine3"]()
        for j in range(4):
            prev_fns["outproj"](j)
        prev_fns = None
        P.barrier()
        if stop_after == 3:
            break

    P.wait_all("sp", ["OUT"] + ["dbg_" + k for k in dbg_out])
    P.emit()
    st.close()
    return nc, P, dbg_out


def host_layout(inp):
    f = lambda k: np.asarray(inp[k], np.float32)
    w_in = f("w_in")[0]
    cols = []
    cols += list(range(0, 1536))
    cols += list(range(1536, 1600)) + list(range(1664, 1728))
    cols += list(range(1600, 1664)) + list(range(1728, 1792))
    cols += list(range(1792, 1920))
    cols += list(range(2048, 2176))
    cols += list(range(2304, 2816))
    w_r = np.zeros((D, NCOLW), np.float32)
    w_r[:, 0:2560] = w_in[:, cols]
    for p in range(4):
        for br in range(3):
            for e in range(2):
                w_r[:, 2560 + 32 * p + br * 2 + e] = w_in[:, 2816 + br * 8 + 2 * p + e]
    w_r[:, TCOL0:TCOL0 + 128] = w_in[:, 1920:2048]
    w_r[:, TCOL0 + 128:TCOL0 + 256] = w_in[:, 2176:2304]
    wax = np.zeros((128, 4, 2, 128), np.float32)
    for ai, key in enumerate(("rg_wa", "rg_wx")):
        w = f(key)[0]
        for c in range(4):
            for hh in range(2):
                wax[hh * 64:(hh + 1) * 64, c, ai, hh * 64:(hh + 1) * 64] = w[2 * c + hh]
    w1 = np.zeros((128, 32, 256), np.float32)
    w1[0:64] = f("cmp_k_w1")[0].reshape(32, 64, 256).transpose(1, 0, 2)
    w1[64:128] = f("cmp_v_w1")[0].reshape(32, 64, 256).transpose(1, 0, 2)
    w2 = np.zeros((128, 2, 2, 64), np.float32)
    w2[:, 0] = f("cmp_k_w2")[0].reshape(2, 128, 64).transpose(1, 0, 2)
    w2[:, 1] = f("cmp_v_w2")[0].reshape(2, 128, 64).transpose(1, 0, 2)
    cf = np.zeros((128, F_END), np.float32)
    cf[:, F_N1:F_N1 + 8] = f("norm1_w")[0].reshape(8, 128).T
    cf[:, F_CONVW:F_CONVW + 16] = f("conv_w")[0].T.reshape(4, 128, 4).transpose(1, 0, 2).reshape(128, 16)
    for off, key in ((F_CONVB, "conv_b"), (F_BA, "rg_ba"), (F_BX, "rg_bx"), (F_LAM, "rg_lambda")):
        cf[:, off:off + 4] = f(key)[0].reshape(4, 128).T
    tl = np.arange(128)[:, None]
    xx = np.arange(64)[None, :]
    jj = xx - 32
    curp = (tl >= 64).astype(np.int64)
    noncausal = jj > curp
    forced = (jj == curp) | (jj == curp - 1)
    cf[:, F_PATK:F_PATK + 64] = np.where(noncausal | forced, 0.0, 1.0)
    cf[:, F_PATA:F_PATA + 64] = np.where(noncausal, -1e9, np.where(forced, 1e9, 0.0))
    cf[0:64, F_PEC:F_PEC + 32] = f("cmp_k_pe")[0].T
    cf[64:128, F_PEC:F_PEC + 32] = f("cmp_v_pe")[0].T
    cf[:, F_ONE] = 1.0
    cf[:, F_HALF] = 0.5
    cf[:, F_MHALF] = -0.5
    cb = np.zeros((128, C_END), np.float32)
    cb[:, C_IDB:C_IDB + 128] = np.eye(128)
    kl = np.arange(128)[:, None]
    ql = np.arange(128)[None, :]
    cb[:, C_TRID:C_TRID + 128] = np.where(kl > ql, NEG, 0.0)
    cb[:, C_TRIA:C_TRIA + 128] = np.where(kl <= ql, NEG, 0.0)
    cs = np.arange(127)[:, None] * 16
    ss = np.arange(32)[None, :] * 64
    ov = np.clip(np.minimum(cs + 32, ss + 64) - np.maximum(cs, ss), 0, None).astype(np.float32) / 32.0
    cb[0:127, C_OV:C_OV + 32] = ov
    selq = np.zeros((128, 2, 3, 128), np.float32)
    for p in range(4):
        for br in range(3):
            for e in range(2):
                selq[32 * p + br * 2 + e, 0, br, 64 * e:64 * e + 64] = 1.0
                if p == 3:
                    selq[32 * p + br * 2 + e, 1, br, 64 * e:64 * e + 64] = 1.0
    cb[:, C_SELQ:C_SELQ + 768] = selq.reshape(128, 768)
    band = np.zeros((32, 160), np.float32)
    for r in range(32):
        band[r, r + 127] = 1.0
    cb[0:32, C_BAND:C_BAND + 160] = band
    cb[64:96, C_BAND:C_BAND + 160] = band
    r_ = np.arange(32)[:, None]
    t_ = np.arange(512)[None, :]
    cb[0:32, C_CMB0:C_CMB0 + 512] = np.where(16 * (r_ - 1) + 31 > t_, NEG, 0.0)
    cb[64:96, C_CMB0:C_CMB0 + 512] = cb[0:32, C_CMB0:C_CMB0 + 512]
    erows = (np.arange(S)[None, :] // 64 == np.arange(32)[:, None]).astype(np.float32)
    return dict(w_in=w_r, w_out=np.ascontiguousarray(f("w_out")[0]), wax=wax.reshape(128, -1), w1=w1.reshape(128, -1),
                w2=w2.reshape(128, -1), nf=f("normf_w"), cb=cb, cf=cf, erows=erows)


_CACHE = {}


def kernel(**inputs):
    x = np.ascontiguousarray(np.asarray(inputs["x"], np.float32))
    shared = host_layout(inputs)
    if "nc" not in _CACHE:
        _CACHE["nc"] = build_program()[0]
    nc = _CACHE["nc"]
    in_maps = [dict(shared, x=x[2 * c:2 * c + 2]) for c in range(8)]
    res = run_bass_kernel_spmd(nc, in_maps, core_ids=list(range(8)))
    return np.concatenate([r["out"] for r in res.results], axis=0)
```

```python
import contextlib
import numpy as np
import concourse.bass as bass
import concourse.mybir as mybir
from concourse.bass_utils import run_bass_kernel_spmd

F32 = mybir.dt.float32
BF16 = mybir.dt.bfloat16
AF = mybir.ActivationFunctionType
ALU = mybir.AluOpType
AX = mybir.AxisListType

S = 2048
D = 1024
NG = 4
NCOLW = 2918
TCOL0 = 2662
NEG = -30000.0

C_IDB, C_TRID, C_TRIA, C_OV, C_SELQ, C_BAND, C_CMB0, C_END = 0, 128, 256, 384, 416, 1184, 1344, 1856
F_N1, F_CONVW, F_CONVB, F_BA, F_BX, F_LAM, F_C8, F_B1, F_PATK, F_PATA, F_PEC, F_ONE, F_HBA, F_HBX, F_C4, F_HB1, F_HALF, F_MHALF, F_END = 0, 8, 24, 28, 32, 36, 40, 44, 48, 112, 176, 208, 216, 220, 224, 228, 232, 233, 240


class Prog:
    ENGS = ("pe", "act", "dve", "pool", "sp")

    def __init__(self, nc):
        self.nc = nc
        self.ops = {e: [] for e in self.ENGS}
        self.W = {}
        self.R = {}
        self.dma_cnt = {}

    def _deps_for(self, eng, reads, writes, is_dma):
        deps = {}

        def add(key, val, kind):
            if key[0] == "e" and key[1] == eng and not is_dma and kind != "raw" and eng == "pe":
                return
            if key not in deps or deps[key] < val:
                deps[key] = val

        for r in reads:
            for key, val in self.W.get(r, {}).items():
                add(key, val, "raw")
        for w in writes:
            for key, val in self.W.get(w, {}).items():
                add(key, val, "waw")
            for key, val in self.R.get(w, {}).items():
                add(key, val, "war")
        return deps

    @staticmethod
    def _l(x):
        return [x] if isinstance(x, str) else list(x)

    def op(self, eng, fn, reads=(), writes=()):
        reads, writes = self._l(reads), self._l(writes)
        deps = self._deps_for(eng, reads, writes, False)
        idx = len(self.ops[eng])
        self.ops[eng].append(dict(fn=fn, deps=deps, dma=None))
        key = ("e", eng)
        for r in reads:
            self.R.setdefault(r, {})[key] = idx
        for w in writes:
            self.R[w] = {}
            self.W.setdefault(w, {})[key] = idx
        return idx

    def dma(self, q, stream, fn, reads=(), writes=()):
        reads, writes = self._l(reads), self._l(writes)
        deps = self._deps_for(q, reads, writes, True)
        self.dma_cnt[stream] = self.dma_cnt.get(stream, 0) + 16
        val = self.dma_cnt[stream]
        self.ops[q].append(dict(fn=fn, deps=deps, dma=stream))
        key = ("d", stream)
        for r in reads:
            self.R.setdefault(r, {})[key] = val
        for w in writes:
            self.R[w] = {}
            self.W.setdefault(w, {})[key] = val
        return val

    def barrier(self):
        deps = {}
        for e in self.ENGS:
            for i in range(len(self.ops[e]) - 1, -1, -1):
                o = self.ops[e][i]
                if o["fn"] is not None and o["dma"] is None:
                    deps[("e", e)] = i
                    break
        for s, v in self.dma_cnt.items():
            deps[("d", s)] = v
        for e in self.ENGS:
            d = {k: v for k, v in deps.items() if k != ("e", e)}
            self.ops[e].append(dict(fn=None, deps=d, dma=None))

    def wait_all(self, eng, resources):
        deps = {}
        for r in resources:
            for key, val in self.W.get(r, {}).items():
                if key not in deps or deps[key] < val:
                    deps[key] = val
        self.ops[eng].append(dict(fn=None, deps=deps, dma=None))

    def emit(self):
        nc = self.nc
        flagged = {e: set() for e in self.ENGS}
        for e in self.ENGS:
            for o in self.ops[e]:
                for key, val in o["deps"].items():
                    if key[0] == "e":
                        flagged[key[1]].add(val)
        cnt = {}
        for e in self.ENGS:
            c = 0
            m = {}
            fl = flagged[e]
            for i in range(len(self.ops[e])):
                if i in fl:
                    c += 1
                m[i] = c
            cnt[e] = m
        self.max_counts = {e: (max(cnt[e].values()) if cnt[e] else 0) for e in self.ENGS}
        sems = {}
        with contextlib.ExitStack() as st:
            for e in self.ENGS:
                sems[("e", e)] = st.enter_context(nc.semaphore("s_" + e))
            for s in self.dma_cnt:
                sems[("d", s)] = st.enter_context(nc.semaphore("d_" + s))
            block = st.enter_context(nc.Block())
            engmap = {"pe": block.tensor, "act": block.scalar, "dve": block.vector,
                      "pool": block.gpsimd, "sp": block.sync}
            for e in self.ENGS:
                ops = self.ops[e]
                if not ops:
                    continue
                fl = flagged[e]

                def body(eng, e=e, ops=ops, fl=fl):
                    seen = {}
                    for i, o in enumerate(ops):
                        for key, val in o["deps"].items():
                            v = cnt[key[1]][val] if key[0] == "e" else val
                            if seen.get(key, 0) >= v:
                                continue
                            seen[key] = v
                            eng.wait_ge(sems[key], v)
                        if o["fn"] is None:
                            continue
                        ins = o["fn"](eng)
                        if o["dma"] is not None:
                            ins.then_inc(sems[("d", o["dma"])], 16)
                        elif i in fl:
                            ins.then_inc(sems[("e", e)], 1)

                engmap[e](body)


class Mem:
    def __init__(self, nc):
        self.nc = nc
        self.base = (nc.sbuf_base + 31) // 32 * 32
        self.top = nc.sbuf_top
        self.cur = self.base
        self.hi = self.base

    def alloc(self, name, shape, dt):
        n = 1
        for s in shape[1:]:
            n *= s
        nbytes = (n * (4 if dt == F32 else 2) + 31) // 32 * 32
        t = self.nc.alloc_sbuf_tensor_at(name, list(shape), dt, offset=self.cur)
        self.cur += nbytes
        self.hi = max(self.hi, self.cur)
        assert self.cur <= self.top, f"SBUF overflow at {name}: {self.cur - self.top} bytes over"
        return t


def build_program(nseq=2, dbg=(), stop_after=None):
    nc = bass.Bass("TRN2", target_bir_lowering=False)
    dram = lambda name, shape, kind="ExternalInput": nc.dram_tensor(name, list(shape), F32, kind=kind).ap()
    x_d = dram("x", [nseq, S, D])
    win_d = dram("w_in", [D, NCOLW])
    wout_d = dram("w_out", [D, D])
    wax_d = dram("wax", [128, 4 * 2 * 128])
    w1_d = dram("w1", [128, 32 * 256])
    w2_d = dram("w2", [128, 2 * 2 * 64])
    nf_d = dram("nf", [D])
    cb_d = dram("cb", [128, C_END])
    cf_d = dram("cf", [128, F_END])
    er_d = dram("erows", [32, S])
    out_d = dram("out", [nseq, S, D], kind="ExternalOutput")
    dbg_out = {}

    P = Prog(nc)
    M = Mem(nc)
    st = contextlib.ExitStack()
    PS = [st.enter_context(nc.psum_tensor(f"ps{b}", [128, 512], F32)) for b in range(7)]
    PSB = st.enter_context(nc.psum_tensor("psb", [128, 8, 128], BF16))

    WIN = M.alloc("WIN", [128, 8, NCOLW], BF16)
    WOUT = M.alloc("WOUT", [128, 8, D], BF16)
    W2 = M.alloc("W2", [128, 2, 2, 64], BF16)
    WAX = M.alloc("WAX", [128, 4, 2, 128], BF16)
    CB = M.alloc("CB", [128, C_END], BF16)
    CF = M.alloc("CF", [128, F_END], F32)
    SMALL = M.alloc("SMALL", [128, 80], F32)
    PEB = M.alloc("PEB", [128, 32], BF16)
    HST = M.alloc("HST", [128, 4], F32)
    IMPT = M.alloc("IMPT", [128, 4, 48], F32)
    MBTOK = M.alloc("MBTOK", [128, 4, 32], BF16)
    RG = M.alloc("RG", [128, 4, S], BF16)
    Q = M.alloc("Q", [128, 4, S], BF16)
    AG = M.alloc("AG", [128, 4, S], BF16)
    KE = M.alloc("KE", [96, 2, S], BF16)
    KW = M.alloc("KW", [128, 2, S], BF16)
    EN = M.alloc("EN", [128, S], F32)
    V = M.alloc("V", [128, 16, 4, 66], BF16)
    KC = M.alloc("KC", [128, 2, 128], BF16)
    VC = M.alloc("VC", [128, 2, 66], BF16)
    ov_base = M.cur

    XT = [M.alloc(f"XT{j}", [128, D], F32) for j in range(2)]
    XS = [M.alloc(f"XS{j}", [128, D], BF16) for j in range(2)]
    HT = [M.alloc(f"HT{j}", [128, 8, 512], BF16) for j in range(2)]
    HIDS = nc.alloc_sbuf_tensor_at("HIDS", [128, 2, 2, 2, 128], BF16, offset=M.cur - 2 * 8192 - 2 * 2048)
    W1 = nc.alloc_sbuf_tensor_at("W1", [128, 32, 256], BF16, offset=M.cur - 2 * 8192)
    RXG = [M.alloc(f"RXG{j}", [128, 4, 515], BF16) for j in range(2)]
    XC = M.alloc("XC", [128, 512], F32)
    XCV = [M.alloc(f"XCV{j}", [128, 512], F32) for j in range(2)]
    XCB = M.alloc("XCB", [128, 512], BF16)
    AA = M.alloc("AA", [128, 512], F32)
    UU = M.alloc("UU", [128, 512], F32)
    TH = M.alloc("TH", [128, 512], BF16)
    KVC = M.alloc("KVC", [128, 2, S], BF16)
    p1_end = M.cur
    M.cur = ov_base
    PT = [M.alloc(f"PT{j}", [128, 512], BF16) for j in range(4)]
    QM = [M.alloc(f"QM{j}", [96, 512], BF16) for j in range(4)]
    OSP = [M.alloc(f"OSP{j}", [128, 3, 512], F32) for j in range(2)]
    OSPC = [M.alloc(f"OSPC{j}", [128, 2, 512], F32) for j in range(2)]
    DENSB = [M.alloc("DENSB", [97, 2, 512], F32)] * 2
    DEN = M.alloc("DEN", [128, 512], F32)
    PRD = M.alloc("PRD", [128, 512], F32)
    SHI = M.alloc("SHI", [128, 512], BF16)
    SLO = M.alloc("SLO", [128, 512], BF16)
    MIXA = M.alloc("MIXA", [128, 4, 512], BF16)
    MBT = M.alloc("MBT", [32, 512], BF16)
    XR = [M.alloc(f"XR{j}", [128, D], F32) for j in range(2)]
    WNF = M.alloc("WNF", [128, D], F32)
    JUNK = M.alloc("JUNK", [128, 512], BF16)
    ov_end = max(M.cur, p1_end)
    M.cur = ov_end

    IDB = CB[:, C_IDB:C_IDB + 128]
    TRID = CB[:, C_TRID:C_TRID + 128]
    TRIA = CB[:, C_TRIA:C_TRIA + 128]

    def dump(name, ap, shape, dt, reads, dst=None):
        if name not in dbg:
            return
        if name not in dbg_out:
            dbg_out[name] = nc.dram_tensor("dbg_" + name, list(shape), dt, kind="ExternalOutput").ap()
        t = dbg_out[name]
        d_ap = t if dst is None else dst(t)
        P.dma("sp", "dbg", lambda e: e.dma_start(out=d_ap, in_=ap), reads=reads, writes="dbg_" + name)

    def mm(out, lhsT, rhs, start, stop, reads, writes, sgc=False):
        P.op("pe", lambda e: e.matmul(out, lhsT=lhsT, rhs=rhs, start=start, stop=stop, skip_group_check=sgc), reads, writes)

    def act(out, in_, func, reads, writes, **kw):
        P.op("act", lambda e: e.activation(out=out, in_=in_, func=func, **kw), reads, writes)

    def vcopy(eng, out, in_, reads, writes):
        P.op(eng, lambda e: e.tensor_copy(out=out, in_=in_), reads, writes)

    def tsc(eng, out, in0, s1, s2, op0, op1, reads, writes):
        if op1 is None:
            P.op(eng, lambda e: e.tensor_scalar(out=out, in0=in0, scalar1=s1, scalar2=None, op0=op0), reads, writes)
        else:
            P.op(eng, lambda e: e.tensor_scalar(out=out, in0=in0, scalar1=s1, scalar2=s2, op0=op0, op1=op1), reads, writes)

    def tt(eng, out, in0, in1, op, reads, writes):
        P.op(eng, lambda e: e.tensor_tensor(out=out, in0=in0, in1=in1, op=op), reads, writes)

    def stt(out, in0, scalar, in1, op0, op1, reads, writes):
        P.op("dve", lambda e: e.scalar_tensor_tensor(out=out, in0=in0, scalar=scalar, in1=in1, op0=op0, op1=op1),
             reads, writes)

    def o_tile(p, br):
        if br == 0:
            g2 = (p // 2) % 2
            return OSPC[g2][:, p % 2, :], f"OSPC{g2}:{p % 2}"
        return OSP[p % 2][:, br, :], f"OSP{p % 2}:{br}"

    def evac_o(osb, p, e, br, ob):
        rows = slice(64 * e, 64 * e + 64)
        if br == 0:
            g2 = (p // 2) % 2
            dst, nm = OSPC[g2][rows, p % 2, :], f"OSPC{g2}:{p % 2}:{e}"
        else:
            dst, nm = OSP[osb][rows, br, :], f"OSP{osb}:{br}:{e}"
        act(dst, PS[ob][0:64, :], AF.Copy, reads=f"ps{ob}", writes=nm)
        dsl = br * 2 + e
        dsp = slice(32 * (dsl % 4), 32 * (dsl % 4) + 1)
        if e == 0:
            act(DENSB[0][dsp, dsl // 4, :], PS[ob][64:65, :], AF.Copy, reads=f"ps{ob}", writes=f"DSB:{dsl}")
        else:
            vcopy("dve", DENSB[0][dsp, dsl // 4, :], PS[ob][64:65, :], reads=f"ps{ob}", writes=f"DSB:{dsl}")
        drow = 32 * p + br * 2 + e
        P.dma("sp", f"den{p}_{br}", lambda en: en.dma_start(out=DEN[drow:drow + 1, :], in_=DENSB[0][dsp, dsl // 4, :]),
              reads=f"DSB:{dsl}", writes=f"DEN{p}")

    bank_rr = {"i": 0}

    P.dma("sp", "c_cf", lambda e: e.dma_start(out=CF[:], in_=cf_d), writes="CF")
    P.dma("pool", "c_cb", lambda e: e.dma_start(out=CB[:], in_=cb_d), writes="CB")
    for g in range(2):
        P.dma("pool", "c_er", (lambda g: lambda e: e.dma_start(out=KE[64:96, g, :], in_=er_d))(g), writes=f"KEE{g}")
    P.dma("pool", "c_wax", lambda e: e.dma_start(out=WAX[:].rearrange("p a b c -> p (a b c)"), in_=wax_d), writes="WAX")
    P.dma("pool", "c_w2", lambda e: e.dma_start(out=W2[:].rearrange("p a b c -> p (a b c)"), in_=w2_d), writes="W2")
    P.op("pool", lambda e: e.memset(V[:, :, :, 64:65], 1.0), writes="Vones")
    P.op("pool", lambda e: e.memset(VC[:, :, 64:65], 1.0), writes="VCones")
    STG0 = nc.alloc_sbuf_tensor_at("STGa", [128, 8, 512], F32, offset=ov_base)
    STG1 = nc.alloc_sbuf_tensor_at("STGb", [128, 8, 512], F32, offset=ov_base + 16384)
    assert ov_base + 32768 <= p1_end
    STGs = [STG0, STG1]
    win_v = win_d.rearrange("(k p) n -> p k n", p=128)
    nblk = (NCOLW + 511) // 512
    for blk in range(nblk):
        c0, c1 = blk * 512, min(NCOLW, blk * 512 + 512)
        sg = STGs[blk % 2]
        P.dma("sp", f"stg{blk % 2}", (lambda sg, c0, c1: lambda e: e.dma_start(out=sg[:, :, 0:c1 - c0], in_=win_v[:, :, c0:c1]))(sg, c0, c1),
              writes=f"STG{blk % 2}")
        for k in range(8):
            eng = "dve" if k % 2 == 0 else "pool"
            tsc(eng, WIN[:, k, c0:c1], sg[:, k, 0:c1 - c0], CF[:, F_N1 + k:F_N1 + k + 1], 0.0, ALU.mult, ALU.add,
                reads=[f"STG{blk % 2}", "CF"], writes=f"WIN{blk}")
    P.dma("pool", "c_wout", lambda e: e.dma_start(out=WOUT[:], in_=wout_d.rearrange("(k p) n -> p k n", p=128)), writes="WOUT")
    EE = SMALL[:, 32:36]
    TT = SMALL[:, 36:40]
    act(EE, CF[:, F_LAM:F_LAM + 4], AF.Exp, reads="CF", writes="EE", scale=-1.0)
    tsc("dve", TT, EE, -0.25, 1.0 / 3.0, ALU.mult, ALU.add, reads="EE", writes="TTs")
    tt("dve", TT, TT, EE, ALU.mult, reads=["TTs", "EE"], writes="TTs")
    tsc("dve", TT, TT, -1.0, 0.5, ALU.mult, ALU.add, reads="TTs", writes="TTs")
    tt("dve", TT, TT, EE, ALU.mult, reads=["TTs", "EE"], writes="TTs")
    tsc("dve", TT, TT, -1.0, 1.0, ALU.mult, ALU.add, reads="TTs", writes="TTs")
    tt("dve", TT, TT, EE, ALU.mult, reads=["TTs", "EE"], writes="TTs")
    tsc("dve", CF[:, F_C8:F_C8 + 4], TT, -8.0, None, ALU.mult, None, reads="TTs", writes="C8")
    tsc("dve", CF[:, F_C4:F_C4 + 4], CF[:, F_C8:F_C8 + 4], 0.5, None, ALU.mult, None, reads="C8", writes="C4")
    tsc("dve", CF[:, F_HBA:F_HBA + 4], CF[:, F_BA:F_BA + 4], 0.5, None, ALU.mult, None, reads="CF", writes="HBA")
    tsc("dve", CF[:, F_HBX:F_HBX + 4], CF[:, F_BX:F_BX + 4], 0.5, None, ALU.mult, None, reads="CF", writes="HBX")
    tsc("dve", WOUT[:].rearrange("p a b -> p (a b)"), WOUT[:].rearrange("p a b -> p (a b)"), 0.5, None, ALU.mult, None, reads="WOUT", writes="WOUT")
    tsc("dve", W2[:].rearrange("p a b c -> p (a b c)"), W2[:].rearrange("p a b c -> p (a b c)"), 0.5, None, ALU.mult, None, reads="W2", writes="W2")
    P.barrier()

    x_t = x_d.rearrange("s (i p) d -> s i p d", p=128)
    out_t = out_d.rearrange("s (i p) d -> s i p d", p=128)

    def chunk_cols(c):
        if c < 20:
            return c * 128, 128
        return 2560, 102

    for s in range(nseq):
        def stage_front(i, s=s):
            j = i % 2
            P.dma("sp", f"xt{j}", (lambda j, i, s: lambda e: e.dma_start(out=XT[j][:], in_=x_t[s, i]))(j, i, s), writes=f"XT{j}")
            ssq = SMALL[:, i:i + 1]
            rs = SMALL[:, 16 + i:16 + i + 1]
            act(XS[j][:], XT[j][:], AF.Square, reads=f"XT{j}", writes=[f"XS{j}", f"ssq{i}"], accum_out=ssq)
            tsc("dve", rs, ssq, 1.0 / D, 1e-6, ALU.mult, ALU.add, reads=f"ssq{i}", writes=f"rs{i}")
            tt("pool", rs, rs, CF[:, F_MHALF:F_MHALF + 1], ALU.pow, reads=[f"rs{i}", "CF"], writes=f"rs{i}")
            tsc("dve", XS[j][:], XT[j][:], rs, None, ALU.mult, None, reads=[f"XT{j}", f"rs{i}"], writes=f"XS{j}")

        def stage_back(i):
            n_, i4 = i // 4, i % 4
            hs = n_ % 2
            j = i % 2
            for k in range(8):
                P.op("pe", (lambda j, k: lambda e: e.transpose(out=PSB[:, k, :], in_=XS[j][:, k * 128:(k + 1) * 128], identity=IDB))(j, k),
                     reads=[f"XS{j}", "CB"], writes="psb")
            act(HT[hs][:, :, i4 * 128:(i4 + 1) * 128], PSB[:], AF.Copy, reads="psb", writes=f"HT{hs}:{i4}")

        def proj_chunk(n, c):
            hs = n % 2
            rb = n % 2
            c_tok = slice(n * 512, (n + 1) * 512)
            c0, cw = chunk_cols(c)
            b = bank_rr["i"] % 6
            bank_rr["i"] += 1
            pb = PS[b]
            htr = [f"HT{hs}:{q}" for q in range(4)]
            for k in range(8):
                mm(pb[0:cw, :], WIN[:, k, c0:c0 + cw], HT[hs][:, k, :], k == 0, k == 7,
                   reads=[f"WIN{c0 // 512}"] + htr, writes=f"ps{b}")
            rd = f"ps{b}"
            if c < 4:
                if c % 2 == 0:
                    act(RXG[rb][:, c, 3:515], pb[:], AF.Copy, reads=rd, writes=f"RX{rb}:{c}")
                else:
                    vcopy("dve", RXG[rb][:, c, 3:515], pb[:], reads=rd, writes=f"RX{rb}:{c}")
            elif c < 8:
                act(TH[:], pb[:], AF.Tanh, reads=rd, writes="TH", scale=0.5)
                stt(RG[:, c - 4, c_tok], TH[:], 1.0, pb[:], ALU.add, ALU.mult, reads=["TH", rd], writes=f"RG{c - 4}:{n}")
            elif c < 12:
                tsc("dve", Q[:, c - 8, c_tok], pb[:], 0.125, None, ALU.mult, None, reads=rd, writes=[f"Q{c - 8}:{n}"])
            elif c < 14:
                vcopy("dve", KVC[:, c - 12, c_tok], pb[:], reads=rd, writes=f"KVC{c - 12}:{n}")
            elif c == 14:
                vcopy("dve", KE[0:64, 0, c_tok], pb[0:64, :], reads=rd, writes=f"KE0:{n}")
                act(KE[0:64, 1, c_tok], pb[64:128, :], AF.Copy, reads=rd, writes=f"KE1:{n}")
            elif c == 15:
                vcopy("dve", KW[0:64, 0, c_tok], pb[0:64, :], reads=rd, writes=f"KW0:{n}")
                act(KW[64:128, 0, c_tok], pb[0:64, :], AF.Copy, reads=rd, writes=f"KW0:{n}")
                vcopy("dve", KW[64:128, 1, c_tok], pb[64:128, :], reads=rd, writes=f"KW1:{n}")
                act(KW[0:64, 1, c_tok], pb[64:128, :], AF.Copy, reads=rd, writes=f"KW1:{n}")
            elif c < 20:
                act(TH[:], pb[:], AF.Tanh, reads=rd, writes="TH", scale=0.5)
                stt(AG[:, c - 16, c_tok], TH[:], 1.0, pb[:], ALU.add, ALU.mult, reads=["TH", rd], writes=[f"AG{c - 16}:{n}"])
            else:
                act(EN[0:102, c_tok], pb[0:102, :], AF.Exp, reads=rd, writes=f"EN:{n}", scale=-1.0)

        def v_part(n):
            hs = n % 2
            for i4 in range(4):
                i = 4 * n + i4
                b = bank_rr["i"] % 6
                bank_rr["i"] += 1
                pb = PS[b]
                for k in range(8):
                    mm(pb[:, 0:256], HT[hs][:, k, i4 * 128:(i4 + 1) * 128], WIN[:, k, TCOL0:TCOL0 + 256], k == 0, k == 7,
                       reads=["WIN5", f"HT{hs}:{i4}"], writes=f"ps{b}")
                vcopy("dve", V[:, i, :, 0:64], pb[:, 0:256].rearrange("p (a b) -> p a b", a=4), reads=f"ps{b}", writes=f"V{i}")

        def halo(n):
            rb = n % 2
            if n == 0:
                P.op("pool", lambda e: e.memset(RXG[0][:, :, 0:3], 0.0), writes=[f"RXh{rb}"])
            else:
                vcopy("pool", RXG[rb][:, :, 0:3], RXG[1 - rb][:, :, 512:515], reads=[f"RX{1 - rb}:{c}" for c in range(4)], writes=f"RXh{rb}")

        def lru_conv(n, c):
            rb = n % 2
            xv = XCV[c % 2]
            rxr = [f"RX{rb}:{c}", f"RXh{rb}"]
            cw = lambda k: CF[:, F_CONVW + 4 * c + k:F_CONVW + 4 * c + k + 1]
            tsc("dve", xv[:], RXG[rb][:, c, 3:515], cw(3), CF[:, F_CONVB + c:F_CONVB + c + 1], ALU.mult, ALU.add,
                reads=rxr + ["CF"], writes=f"XCV{c % 2}")
            for k in (2, 1, 0):
                stt(xv[:], RXG[rb][:, c, k:512 + k], cw(k), xv[:], ALU.mult, ALU.add, reads=rxr + ["CF", f"XCV{c % 2}"], writes=f"XCV{c % 2}")

        def lru_cast(n, c):
            vcopy("pool", XCB[:], XCV[c % 2][:], reads=f"XCV{c % 2}", writes="XCB")

        def lru_a(n, c):
            lru_conv(n, c)
            lru_cast(n, c)

        def lru_b(n, c, s=s):
            c_tok = slice(n * 512, (n + 1) * 512)
            for ai, (dst, boff, nm) in enumerate(((AA, F_HBA, "AA"), (UU, F_HBX, "UU"))):
                if ai == 0:
                    b = 6
                else:
                    b = bank_rr["i"] % 6
                    bank_rr["i"] += 1
                mm(PS[b][:, :], WAX[:, c, ai, :], XCB[:], True, True, reads=["WAX", "XCB"], writes=f"ps{b}")
                act(dst[:], PS[b][:, :], AF.Tanh, reads=[f"ps{b}", "HBA", "HBX"], writes=nm, scale=0.5, bias=CF[:, boff + c:boff + c + 1])
            act(AA[:], AA[:], AF.Exp, reads=["AA", "C4"], writes="AA", scale=CF[:, F_C4 + c:F_C4 + c + 1], bias=CF[:, F_C4 + c:F_C4 + c + 1])
            stt(UU[:], UU[:], 1.0, XCV[c % 2][:], ALU.add, ALU.mult, reads=["UU", f"XCV{c % 2}"], writes="UU")
            tt("pool", XC[:], AA[:], AA[:], ALU.mult, reads=["AA"], writes="XC")
            tsc("pool", XC[:], XC[:], -1.0, 1.0, ALU.mult, ALU.add, reads="XC", writes="XC")
            act(XC[:], XC[:], AF.Sqrt, reads="XC", writes="XC")
            stt(UU[:], UU[:], 0.5, XC[:], ALU.mult, ALU.mult, reads=["UU", "XC"], writes="UU")
            init = 0.0 if n == 0 else HST[:, c:c + 1]
            P.op("dve", (lambda init: lambda e: e.tensor_tensor_scan(out=XC[:], data0=AA[:], data1=UU[:], initial=init, op0=ALU.mult, op1=ALU.add))(init),
                 reads=["AA", "UU", f"HST{c}"], writes="XC")
            vcopy("dve", HST[:, c:c + 1], XC[:, 511:512], reads="XC", writes=f"HST{c}")
            if s == 0 and c == 0:
                dump("LRU0", XC[:], [128, S], F32, ["XC"], dst=(lambda c_tok: lambda t: t[:, c_tok])(c_tok))
            tt("dve", RG[:, c, c_tok], XC[:], RG[:, c, c_tok], ALU.mult, reads=["XC", f"RG{c}:{n}"], writes=f"RG{c}:{n}")

        stage_front(0)
        for i in range(4):
            stage_front(i + 1)
            stage_back(i)
        for n in range(NG):
            halo(n)
            for c in range(21):
                proj_chunk(n, c)
                if n > 0:
                    if c == 0:
                        lru_a(n - 1, 0)
                    if c in (3, 8, 13, 18):
                        t_ = (c - 3) // 5
                        if t_ < 3:
                            lru_conv(n - 1, t_ + 1)
                        lru_b(n - 1, t_)
                        if t_ < 3:
                            lru_cast(n - 1, t_ + 1)
                if n < NG - 1 and c in (3, 8, 13, 18):
                    ti = 4 * (n + 1) + (c - 3) // 5
                    stage_back(ti)
                    if ti + 1 < 16:
                        stage_front(ti + 1)
            v_part(n)
        P.dma("pool", "c_w1", lambda e: e.dma_start(out=W1[:].rearrange("p a b -> p (a b)"), in_=w1_d),
              writes=["W1"] + [f"HT{a}:{q}" for a in range(2) for q in range(4)])
        if s == 0:
            vcopy("dve", PEB[:], CF[:, F_PEC:F_PEC + 32], reads="CF", writes="PEB")
            for kv in range(2):
                rows = slice(64 * kv, 64 * kv + 64)
                b = 2 * kv
                for hc in range(2):
                    col = kv * 2 + hc
                    for l in range(32):
                        mm(PS[b][:, col:col + 1], W1[rows, l, hc * 128:(hc + 1) * 128], PEB[rows, l:l + 1], l == 0, l == 31,
                           reads=["W1", "PEB"], writes=f"ps{b}")
                vcopy("dve", CF[:, F_B1 + 2 * kv:F_B1 + 2 * kv + 2], PS[b][:, 2 * kv:2 * kv + 2], reads=f"ps{b}", writes="B1")
                tsc("dve", CF[:, F_HB1 + 2 * kv:F_HB1 + 2 * kv + 2], PS[b][:, 2 * kv:2 * kv + 2], 0.5, None, ALU.mult, None, reads=f"ps{b}", writes="HB1")
        kvc_all = [f"KVC{g}:{n}" for g in range(2) for n in range(NG)]
        blk_i = 0
        for kv in range(2):
            rows = slice(64 * kv, 64 * kv + 64)
            for g in range(2):
                for hc in range(2):
                    b = 2 * kv + hc
                    for l in range(32):
                        mm(PS[b][:, 0:127], W1[rows, l, hc * 128:(hc + 1) * 128], KVC[rows, g, l:l + 2017:16], l == 0, l == 31,
                           reads=["W1"] + kvc_all, writes=f"ps{b}")
                    bcol = kv * 2 + hc
                    act(TH[:, 0:127], PS[b][:, 0:127], AF.Tanh, reads=[f"ps{b}", "HB1"], writes="TH", scale=0.5,
                        bias=CF[:, F_HB1 + bcol:F_HB1 + bcol + 1])
                    tsc("dve", TH[:, 0:127], TH[:, 0:127], 1.0, None, ALU.add, None, reads="TH", writes="TH")
                    stt(HIDS[:, kv, g, hc, 0:127], PS[b][:, 0:127], CF[:, F_B1 + bcol:F_B1 + bcol + 1], TH[:, 0:127], ALU.add, ALU.mult,
                        reads=[f"ps{b}", "B1", "TH"], writes=[f"HIDS{kv}{g}{hc}", "XS0"])
                    if blk_i % 2 == 0:
                        lru_a(NG - 1, blk_i // 2)
                    else:
                        lru_b(NG - 1, blk_i // 2)
                    blk_i += 1
        for g in range(2):
            b = 0
            for hc in range(2):
                mm(PS[b][0:64, 0:127], W2[:, 0, hc, :], HIDS[:, 0, g, hc, 0:127], hc == 0, hc == 1,
                   reads=["W2", f"HIDS0{g}{hc}"], writes=f"ps{b}")
            vcopy("dve", KC[0:64, g, 0:127], PS[b][0:64, 0:127], reads=f"ps{b}", writes=f"KC{g}")
            act(KC[64:128, g, 0:127], PS[b][0:64, 0:127], AF.Copy, reads=f"ps{b}", writes=f"KC{g}")
            b = 1
            for hc in range(2):
                mm(PS[b][0:127, 0:64], HIDS[:, 1, g, hc, 0:127], W2[:, 1, hc, :], hc == 0, hc == 1,
                   reads=["W2", f"HIDS1{g}{hc}"], writes=f"ps{b}")
            vcopy("dve", VC[0:127, g, 0:64], PS[b][0:127, 0:64], reads=f"ps{b}", writes=f"VC{g}")
        P.barrier()
        if s == 0:
            allq = lambda nm, cnt: [f"{nm}{c}:{n}" for c in range(cnt) for n in range(NG)]
            dump("KC", KC[:], [128, 2, 128], BF16, ["KC0", "KC1"])
            dump("VC", VC[:], [128, 2, 66], BF16, ["VC0", "VC1", "VCones"])
            dump("RNN", RG[:], [128, 4, S], BF16, allq("RG", 4))
            dump("Q", Q[:], [128, 4, S], BF16, allq("Q", 4))
            dump("AG", AG[:], [128, 4, S], BF16, allq("AG", 4))
            dump("WOUT", WOUT[:], [128, 8, D], BF16, ["WOUT"])
            dump("V", V[:], [128, 16, 4, 66], BF16, [f"V{i}" for i in range(16)])
        if stop_after == 2:
            break
        P.op("pool", lambda e: e.memset(DEN[:], 1.0), writes=[f"DEN{p}" for p in range(4)])
        P.op("pool", lambda e: e.memset(SHI[:], 0.0), writes=["SHI"] + [f"SHI{p}" for p in range(4)])
        P.op("pool", lambda e: e.memset(SLO[:], 0.0), writes=["SLO"] + [f"SLO{p}" for p in range(4)])
        P.op("pool", lambda e: e.memset(PRD[:], 1.0), writes=[f"PRD{p}" for p in range(4)])
        P.dma("sp", "c_nf", lambda e: e.dma_start(out=WNF[:], in_=nf_d.partition_broadcast(128)), writes="WNF")
        rr = {"s": 0, "o": 0, "pt": 0, "qm": 0}

        def nxt(key, mod):
            v = rr[key] % mod
            rr[key] += 1
            return v

        prev_fns = None
        for n in range(NG):
            T0 = n * 512
            tk = slice(T0, T0 + 512)
            ncv = min(127, 32 * n + 31)
            bs0 = 128 - 32 * n

            def cmp_item(g, r4, n=n, T0=T0, tk=tk, ncv=ncv, bs0=bs0):
                h = 4 * g + r4
                p, e = h // 2, h % 2
                rows = slice(64 * e, 64 * e + 64)
                st_ = {}

                def qk():
                    sb = st_["sb"] = nxt("s", 3)
                    mm(PS[sb][0:ncv, :], KC[rows, g, 0:ncv], Q[rows, p, tk], True, False,
                       reads=[f"KC{g}", f"Q{p}:{n}"], writes=f"ps{sb}")
                    mm(PS[sb][0:ncv, :], CB[rows, C_BAND + bs0:C_BAND + bs0 + ncv], CB[rows, C_CMB0:C_CMB0 + 512], False, True,
                       reads=["CB"], writes=f"ps{sb}")
                    pt = st_["pt"] = nxt("pt", 4)
                    act(PT[pt][0:ncv, :], PS[sb][0:ncv, :], AF.Exp, reads=f"ps{sb}", writes=f"PT{pt}")

                def pv():
                    pt = st_["pt"]
                    ob = 3 + nxt("o", 2)
                    mm(PS[ob][0:65, :], VC[0:ncv, g, 0:65], PT[pt][0:ncv, :], True, True,
                       reads=[f"VC{g}", "VCones", f"PT{pt}"], writes=f"ps{ob}")
                    evac_o(p % 2, p, e, 0, ob)
                    for j in range(4):
                        mm(PS[6][:, (j * 4 + r4) * 32:(j * 4 + r4) * 32 + 32], PT[pt][0:ncv, j * 128:(j + 1) * 128], CB[0:ncv, C_OV:C_OV + 32],
                           True, True, reads=[f"PT{pt}", "CB"], writes="ps6")
                return qk, pv

            def imp_dve(g, n=n):
                ib = 6
                psv = PS[ib][:, :]
                rd = SMALL[:, 52:68]
                P.op("dve", lambda en: en.tensor_reduce(out=rd, in_=psv.rearrange("p (a b) -> p a b", a=16), axis=AX.X, op=ALU.add),
                     reads=f"ps{ib}", writes="rd")
                tsc("dve", rd, rd, 1e-30, None, ALU.max, None, reads="rd", writes="rd")
                P.op("dve", lambda en: en.reciprocal(out=rd, in_=rd), reads="rd", writes="rd")
                for j in range(4):
                    i = 4 * n + j
                    x0 = 32 - 2 * i
                    blkv = PS[ib][:, j * 128:(j + 1) * 128]
                    im = IMPT[:, j, 0:32]
                    tsc("dve", im, blkv[:, 0:32], rd[:, 4 * j:4 * j + 1], None, ALU.mult, None, reads=[f"ps{ib}", "rd"], writes=f"im{j}")
                    for r4 in range(1, 4):
                        stt(im, blkv[:, 32 * r4:32 * r4 + 32], rd[:, 4 * j + r4:4 * j + r4 + 1], im, ALU.mult, ALU.add,
                            reads=[f"ps{ib}", "rd", f"im{j}"], writes=f"im{j}")
                    if s == 0:
                        dump("IMP", im, [128, 16, 2, 32], F32, [f"im{j}"], dst=(lambda i, g: lambda t: t[:, i, g, :])(i, g))
                    tt("dve", im, im, CF[:, F_PATK + x0:F_PATK + x0 + 32], ALU.mult, reads=[f"im{j}", "CF"], writes=f"im{j}")
                    tt("dve", im, im, CF[:, F_PATA + x0:F_PATA + x0 + 32], ALU.add, reads=[f"im{j}", "CF"], writes=f"im{j}")
                    P.op("dve", (lambda j: lambda en: en.memset(IMPT[:, j, 0:1], 1e9))(j), reads=f"im{j}", writes=f"im{j}")
                    t8 = IMPT[:, j, 32:40]
                    P.op("dve", (lambda t8, im: lambda en: en.max(out=t8, in_=im))(t8, im), reads=f"im{j}", writes=f"t8{j}")
                    thr = IMPT[:, j, 40:41]
                    tsc("dve", thr, t8[:, 7:8], -5e8, None, ALU.max, None, reads=f"t8{j}", writes=f"thr{j}")
                    tsc("dve", MBTOK[:, j, :], im, thr, NEG, ALU.is_lt, ALU.mult, reads=[f"im{j}", f"thr{j}"], writes=f"MBTOK{j}")

            def mbt_pe(g, n=n, tk=tk):
                for j in range(4):
                    P.op("pe", (lambda j: lambda en: en.transpose(out=PSB[0:32, j, :], in_=MBTOK[:, j, :], identity=IDB))(j),
                         reads=[f"MBTOK{j}", "CB"], writes="psb")
                vcopy("dve", MBT[0:32, :], PSB[0:32, 0:4, :], reads="psb", writes="MBT")
                for p_ in (2 * g, 2 * g + 1):
                    for e_ in range(2):
                        qm = 2 * (p_ % 2) + e_
                        vcopy("pool" if e_ == 0 else "dve", QM[qm][0:64, :], Q[64 * e_:64 * e_ + 64, p_, tk], reads=f"Q{p_}:{n}", writes=f"QM{qm}")
                        vcopy("dve", QM[qm][64:96, :], MBT[0:32, :], reads="MBT", writes=f"QM{qm}")

            def sel_items(p, e, n=n, T0=T0, tk=tk):
                g = p // 2
                rows = slice(64 * e, 64 * e + 64)
                nkc = 4 * n + 4
                st_ = {}
                items = []
                for kc in range(nkc):
                    c0 = max(0, 128 * kc - T0)
                    diag = kc >= 4 * n
                    it = {}

                    def qk(kc=kc, c0=c0, diag=diag, it=it):
                        qm = 2 * (p % 2) + e
                        sb = nxt("s", 3)
                        mm(PS[sb][:, c0:512], KE[0:96, g, kc * 128:(kc + 1) * 128], QM[qm][0:96, c0:512], True, not diag,
                           reads=[f"KE{g}:{kc // 4}", f"KEE{g}", f"QM{qm}"], writes=f"ps{sb}")
                        if diag:
                            mm(PS[sb][:, c0:c0 + 128], IDB, TRID, False, True, reads=["CB"], writes=f"ps{sb}")
                        pt = it["pt"] = nxt("pt", 4)
                        act(PT[pt][:, c0:512], PS[sb][:, c0:512], AF.Exp, reads=f"ps{sb}", writes=f"PT{pt}")

                    def pv(kc=kc, c0=c0, it=it):
                        if kc == 0:
                            st_["ob"] = 3 + nxt("o", 2)
                        ob = st_["ob"]
                        pt = it["pt"]
                        mm(PS[ob][0:65, c0:512], V[:, kc, g, 0:65], PT[pt][:, c0:512], kc == 0, kc == nkc - 1,
                           reads=[f"V{kc}", "Vones", f"PT{pt}"], writes=f"ps{ob}")
                        if kc == nkc - 1:
                            evac_o(p % 2, p, e, 1, ob)
                    items.append((qk, pv))
                return items

            def win_items(p, e, n=n, T0=T0):
                g = p // 2
                rows = slice(64 * e, 64 * e + 64)
                kcs = list(range(max(0, 4 * n - 4), 4 * n + 4))
                st_ = {}
                items = []
                for kc in kcs:
                    i_lo, i_hi = max(kc, 4 * n), min(kc + 4, 4 * n + 3)
                    c0, c1 = (i_lo - 4 * n) * 128, (i_hi - 4 * n + 1) * 128
                    diag = kc >= 4 * n
                    anti = (kc + 4 >= 4 * n) and (kc + 4 <= 4 * n + 3)
                    it = {}

                    def qk(kc=kc, c0=c0, c1=c1, diag=diag, anti=anti, it=it):
                        sb = nxt("s", 3)
                        mm(PS[sb][:, c0:c1], KW[rows, g, kc * 128:(kc + 1) * 128], Q[rows, p, T0 + c0:T0 + c1], True, not (diag or anti),
                           reads=[f"KW{g}:{kc // 4}", f"Q{p}:{n}"], writes=f"ps{sb}")
                        if diag:
                            mm(PS[sb][:, c0:c0 + 128], IDB, TRID, False, True, reads=["CB"], writes=f"ps{sb}")
                        if anti:
                            mm(PS[sb][:, c1 - 128:c1], IDB, TRIA, False, True, reads=["CB"], writes=f"ps{sb}")
                        pt = it["pt"] = nxt("pt", 4)
                        act(PT[pt][:, c0:c1], PS[sb][:, c0:c1], AF.Exp, reads=f"ps{sb}", writes=f"PT{pt}")

                    def pv(kc=kc, c0=c0, c1=c1, it=it):
                        if kc == kcs[0]:
                            st_["ob"] = 3 + nxt("o", 2)
                        ob = st_["ob"]
                        pt = it["pt"]
                        mm(PS[ob][0:65, c0:c1], V[:, kc, 2 + g, 0:65], PT[pt][:, c0:c1], kc == kcs[0], kc == kcs[-1],
                           reads=[f"V{kc}", "Vones", f"PT{pt}"], writes=f"ps{ob}", sgc=True)
                        if kc == kcs[-1]:
                            evac_o(p % 2, p, e, 2, ob)
                    items.append((qk, pv))
                return items

            def run_stream(items, look=2):
                pend = []
                for it in items:
                    if len(it) > 2:
                        for q_ in pend:
                            q_[1]()
                        pend = []
                    if it[0] is not None:
                        it[0]()
                    if it[1] is not None:
                        pend.append(it)
                    while len(pend) > look:
                        pend.pop(0)[1]()
                for q_ in pend:
                    q_[1]()

            def combine_front(p, n=n, T0=T0, tk=tk):
                win = slice(32 * p, 32 * p + 6)
                tsc("pool", DEN[win, :], DEN[win, :], 3e38, 1e-18, ALU.min, ALU.max, reads=f"DEN{p}", writes=f"DEN{p}")
                stt(PRD[win, :], EN[win, tk], 1.0, DEN[win, :], ALU.add, ALU.mult, reads=[f"EN:{n}", f"DEN{p}"], writes=f"PRD{p}")
                P.op("dve", (lambda win: lambda en: en.reciprocal(out=PRD[win, :], in_=PRD[win, :]))(win), reads=[f"PRD{p}"], writes=[f"PRD{p}"])
                vcopy("dve", SHI[win, :], PRD[win, :], reads=f"PRD{p}", writes=f"SHI{p}")
                tt("dve", SLO[win, :], PRD[win, :], SHI[win, :], ALU.subtract, reads=[f"PRD{p}", f"SHI{p}"], writes=f"SLO{p}")
                if s == 0 and n == 0:
                    dump("SG", PRD[win, :], [128, 512], F32, [f"PRD{p}"], dst=(lambda win: lambda t: t[win, :])(win))

            def combine(p, n=n, T0=T0, tk=tk):
                osb = p % 2
                win = slice(32 * p, 32 * p + 6)
                if p < 3:
                    kw_, sset = win, 0
                else:
                    kw_, sset = slice(64, 102), 1
                for br in range(3):
                    bb = 5 + (br % 2)
                    ot, onm = o_tile(p, br)
                    ors = [onm + ":0", onm + ":1"]
                    sel_ap = CB[kw_, C_SELQ + (sset * 3 + br) * 128:C_SELQ + (sset * 3 + br) * 128 + 128]
                    mm(PS[bb][:, :], sel_ap, SHI[kw_, :], True, False, reads=["CB", f"SHI{p}", "SHI"], writes=f"ps{bb}")
                    mm(PS[bb][:, :], sel_ap, SLO[kw_, :], False, True, reads=["CB", f"SLO{p}", "SLO"], writes=f"ps{bb}")
                    tt("dve", ot, ot, PS[bb][:, :], ALU.mult, reads=ors + [f"ps{bb}"], writes=ors)
                    if s == 0:
                        dump("OBR", ot, [128, 3, 4, S], F32, ors, dst=(lambda br, p, tk: lambda t: t[:, br, p, tk])(br, p, tk))
                t0, n0 = o_tile(p, 0)
                t1, n1 = o_tile(p, 1)
                t2, n2 = o_tile(p, 2)
                r0, r1, r2 = [n0 + ":0", n0 + ":1"], [n1 + ":0", n1 + ":1"], [n2 + ":0", n2 + ":1"]
                tt("pool", t1, t1, t2, ALU.add, reads=r1 + r2, writes=r1)
                tt("pool", t1, t1, t0, ALU.add, reads=r1 + r0, writes=r1)
                if s == 0:
                    dump("ATT", t1, [128, 4, S], F32, r1, dst=(lambda p, tk: lambda t: t[:, p, tk])(p, tk))
                tt("pool", MIXA[:, p, :], t1, AG[:, p, tk], ALU.mult, reads=r1 + [f"AG{p}:{n}"], writes=f"MIXA{p}")
                if s == 0:
                    dump("MIXA", MIXA[:, p, :], [128, 4, S], BF16, [f"MIXA{p}"], dst=(lambda p, tk: lambda t: t[:, p, tk])(p, tk))

            def xr_load(j, n=n):
                i = 4 * n + j
                xb = i % 2
                P.dma("sp", f"xr{xb}", (lambda xb, i, s: lambda en: en.dma_start(out=XR[xb][:], in_=x_t[s, i]))(xb, i, s), writes=f"XR{xb}")

            def outproj_tile(j, n=n, T0=T0, xr_load_=xr_load):
                i = 4 * n + j
                xb = i % 2
                if s == 0:
                    dump("XRIN", XR[xb][:], [16, 128, D], F32, [f"XR{xb}"], dst=(lambda i: lambda t: t[i])(i))
                for half in range(2):
                    yb = 5 + half
                    for k in range(8):
                        if k < 4:
                            lhs = RG[:, k, T0 + j * 128:T0 + (j + 1) * 128]
                            rdn = [f"RG{k}:{n}"]
                        else:
                            lhs = MIXA[:, k - 4, j * 128:(j + 1) * 128]
                            rdn = [f"MIXA{k - 4}"]
                        mm(PS[yb][:, :], lhs, WOUT[:, k, half * 512:(half + 1) * 512], k == 0, k == 7,
                           reads=rdn + ["WOUT"], writes=f"ps{yb}")
                    tt("dve", XR[xb][:, half * 512:(half + 1) * 512], XR[xb][:, half * 512:(half + 1) * 512], PS[yb][:, :], ALU.add,
                       reads=[f"XR{xb}", f"ps{yb}"], writes=f"XR{xb}")
                ssq = SMALL[:, 44 + xb:45 + xb]
                rs = SMALL[:, 46 + xb:47 + xb]
                if s == 0:
                    dump("Y", XR[xb][:], [16, 128, D], F32, [f"XR{xb}"], dst=(lambda i: lambda t: t[i])(i))
                ssq2 = SMALL[:, 48 + xb:49 + xb]
                act(JUNK[:], XR[xb][:, 0:512], AF.Square, reads=f"XR{xb}", writes=["JUNK", f"fssq{xb}"], accum_out=ssq)
                act(JUNK[:], XR[xb][:, 512:1024], AF.Square, reads=f"XR{xb}", writes=["JUNK", f"fssqb{xb}"], accum_out=ssq2)
                tt("dve", ssq, ssq, ssq2, ALU.add, reads=[f"fssq{xb}", f"fssqb{xb}"], writes=f"fssq{xb}")
                tsc("dve", rs, ssq, 1.0 / D, 1e-6, ALU.mult, ALU.add, reads=f"fssq{xb}", writes=f"frs{xb}")
                tt("pool", rs, rs, CF[:, F_MHALF:F_MHALF + 1], ALU.pow, reads=[f"frs{xb}", "CF"], writes=f"frs{xb}")
                if s == 0:
                    dump("RSF", SMALL[:, 44:48], [16, 128, 4], F32, [f"frs{xb}", f"fssq{xb}"], dst=(lambda i: lambda t: t[i])(i))
                stt(XR[xb][:], XR[xb][:], rs, WNF[:], ALU.mult, ALU.mult, reads=[f"XR{xb}", f"frs{xb}", "WNF"], writes=f"XR{xb}")
                P.dma("sp", f"st{xb}", (lambda xb, i, s: lambda en: en.dma_start(out=out_t[s, i], in_=XR[xb][:]))(xb, i, s), reads=f"XR{xb}", writes="OUT")
                if j < 2:
                    xr_load_(j + 2)

            def special(fn):
                return (fn, None)

            def with_front(sel_list, nxt_list, p):
                k, real = 0, 0
                while k < len(nxt_list) and real < 3:
                    if nxt_list[k][1] is not None:
                        real += 1
                    k += 1
                return sel_list + nxt_list[:k] + [special((lambda p=p: combine_front(p)))] + nxt_list[k:]

            def real_split(lst, k):
                i_, real = 0, 0
                while i_ < len(lst) and real < k:
                    if lst[i_][1] is not None:
                        real += 1
                    i_ += 1
                return i_

            items = []
            has_prev = prev_fns is not None
            items += [cmp_item(0, r4) for r4 in range(4)]
            items.append(((lambda: imp_dve(0)), None, True))
            w0 = win_items(0, 0) + win_items(0, 1)
            pre = [special(prev_fns["front3"]), special(prev_fns["xr01"])] if has_prev else []
            items += w0[:3] + pre + w0[3:]
            w1a = win_items(1, 0)
            w1 = w1a + win_items(1, 1)
            k6 = real_split(w1, min(6, len(w1a) - 1))
            items += w1[:k6] + ([special(prev_fns["combine3"])] if has_prev else []) + w1[k6:]
            items.append(special(lambda: mbt_pe(0)))
            items += sel_items(0, 0) + sel_items(0, 1)
            sel1 = sel_items(1, 0) + sel_items(1, 1)
            outp = [special((lambda f, j: lambda: f(j))(prev_fns["outproj"], j)) for j in range(4)] if has_prev else []
            items += sel1[:3] + [special(lambda: combine_front(0))] + outp + sel1[3:]
            items.append(special(lambda: combine(0)))
            c1 = [cmp_item(1, r4) for r4 in range(4)]
            items += c1[:3] + [special(lambda: combine_front(1))] + c1[3:]
            items.append(((lambda: imp_dve(1)), None, True))
            items += win_items(2, 0) + win_items(2, 1)
            w3a = win_items(3, 0)
            w3 = w3a + win_items(3, 1)
            k6 = real_split(w3, min(6, len(w3a) - 1))
            items += w3[:k6] + [special(lambda: combine(1))] + w3[k6:]
            items.append(special(lambda: mbt_pe(1)))
            items += sel_items(2, 0) + sel_items(2, 1)
            sel3 = sel_items(3, 0) + sel_items(3, 1)
            items += sel3[:3] + [special(lambda: combine_front(2))] + sel3[3:]
            items.append(special(lambda: combine(2)))
            run_stream(items)
            prev_fns = {"front3": (lambda c=combine_front: c(3)), "combine3": (lambda c=combine: c(3)), "outproj": outproj_tile,
                        "xr01": (lambda f=xr_load: (f(0), f(1)))}
        prev_fns["front3"]()
        prev_fns["xr01"]()
        prev_fns["combine3"]()
        for j in range(4):
            prev_fns["outproj"](j)
        prev_fns = None
        P.barrier()
        if stop_after == 3:
            break

    P.wait_all("sp", ["OUT"] + ["dbg_" + k for k in dbg_out])
    P.emit()
    st.close()
    return nc, P, dbg_out


def host_layout(inp):
    f = lambda k: np.asarray(inp[k], np.float32)
    w_in = f("w_in")[0]
    cols = []
    cols += list(range(0, 1536))
    cols += list(range(1536, 1600)) + list(range(1664, 1728))
    cols += list(range(1600, 1664)) + list(range(1728, 1792))
    cols += list(range(1792, 1920))
    cols += list(range(2048, 2176))
    cols += list(range(2304, 2816))
    w_r = np.zeros((D, NCOLW), np.float32)
    w_r[:, 0:2560] = w_in[:, cols]
    for p in range(4):
        for br in range(3):
            for e in range(2):
                w_r[:, 2560 + 32 * p + br * 2 + e] = w_in[:, 2816 + br * 8 + 2 * p + e]
    w_r[:, TCOL0:TCOL0 + 128] = w_in[:, 1920:2048]
    w_r[:, TCOL0 + 128:TCOL0 + 256] = w_in[:, 2176:2304]
    wax = np.zeros((128, 4, 2, 128), np.float32)
    for ai, key in enumerate(("rg_wa", "rg_wx")):
        w = f(key)[0]
        for c in range(4):
            for hh in range(2):
                wax[hh * 64:(hh + 1) * 64, c, ai, hh * 64:(hh + 1) * 64] = w[2 * c + hh]
    w1 = np.zeros((128, 32, 256), np.float32)
    w1[0:64] = f("cmp_k_w1")[0].reshape(32, 64, 256).transpose(1, 0, 2)
    w1[64:128] = f("cmp_v_w1")[0].reshape(32, 64, 256).transpose(1, 0, 2)
    w2 = np.zeros((128, 2, 2, 64), np.float32)
    w2[:, 0] = f("cmp_k_w2")[0].reshape(2, 128, 64).transpose(1, 0, 2)
    w2[:, 1] = f("cmp_v_w2")[0].reshape(2, 128, 64).transpose(1, 0, 2)
    cf = np.zeros((128, F_END), np.float32)
    cf[:, F_N1:F_N1 + 8] = f("norm1_w")[0].reshape(8, 128).T
    cf[:, F_CONVW:F_CONVW + 16] = f("conv_w")[0].T.reshape(4, 128, 4).transpose(1, 0, 2).reshape(128, 16)
    for off, key in ((F_CONVB, "conv_b"), (F_BA, "rg_ba"), (F_BX, "rg_bx"), (F_LAM, "rg_lambda")):
        cf[:, off:off + 4] = f(key)[0].reshape(4, 128).T
    tl = np.arange(128)[:, None]
    xx = np.arange(64)[None, :]
    jj = xx - 32
    curp = (tl >= 64).astype(np.int64)
    noncausal = jj > curp
    forced = (jj == curp) | (jj == curp - 1)
    cf[:, F_PATK:F_PATK + 64] = np.where(noncausal | forced, 0.0, 1.0)
    cf[:, F_PATA:F_PATA + 64] = np.where(noncausal, -1e9, np.where(forced, 1e9, 0.0))
    cf[0:64, F_PEC:F_PEC + 32] = f("cmp_k_pe")[0].T
    cf[64:128, F_PEC:F_PEC + 32] = f("cmp_v_pe")[0].T
    cf[:, F_ONE] = 1.0
    cf[:, F_HALF] = 0.5
    cf[:, F_MHALF] = -0.5
    cb = np.zeros((128, C_END), np.float32)
    cb[:, C_IDB:C_IDB + 128] = np.eye(128)
    kl = np.arange(128)[:, None]
    ql = np.arange(128)[None, :]
    cb[:, C_TRID:C_TRID + 128] = np.where(kl > ql, NEG, 0.0)
    cb[:, C_TRIA:C_TRIA + 128] = np.where(kl <= ql, NEG, 0.0)
    cs = np.arange(127)[:, None] * 16
    ss = np.arange(32)[None, :] * 64
    ov = np.clip(np.minimum(cs + 32, ss + 64) - np.maximum(cs, ss), 0, None).astype(np.float32) / 32.0
    cb[0:127, C_OV:C_OV + 32] = ov
    selq = np.zeros((128, 2, 3, 128), np.float32)
    for p in range(4):
        for br in range(3):
            for e in range(2):
                selq[32 * p + br * 2 + e, 0, br, 64 * e:64 * e + 64] = 1.0
                if p == 3:
                    selq[32 * p + br * 2 + e, 1, br, 64 * e:64 * e + 64] = 1.0
    cb[:, C_SELQ:C_SELQ + 768] = selq.reshape(128, 768)
    band = np.zeros((32, 160), np.float32)
    for r in range(32):
        band[r, r + 127] = 1.0
    cb[0:32, C_BAND:C_BAND + 160] = band
    cb[64:96, C_BAND:C_BAND + 160] = band
    r_ = np.arange(32)[:, None]
    t_ = np.arange(512)[None, :]
    cb[0:32, C_CMB0:C_CMB0 + 512] = np.where(16 * (r_ - 1) + 31 > t_, NEG, 0.0)
    cb[64:96, C_CMB0:C_CMB0 + 512] = cb[0:32, C_CMB0:C_CMB0 + 512]
    erows = (np.arange(S)[None, :] // 64 == np.arange(32)[:, None]).astype(np.float32)
    return dict(w_in=w_r, w_out=np.ascontiguousarray(f("w_out")[0]), wax=wax.reshape(128, -1), w1=w1.reshape(128, -1),
                w2=w2.reshape(128, -1), nf=f("normf_w"), cb=cb, cf=cf, erows=erows)


_CACHE = {}


def kernel(**inputs):
    x = np.ascontiguousarray(np.asarray(inputs["x"], np.float32))
    shared = host_layout(inputs)
    if "nc" not in _CACHE:
        _CACHE["nc"] = build_program()[0]
    nc = _CACHE["nc"]
    in_maps = [dict(shared, x=x[2 * c:2 * c + 2]) for c in range(8)]
    res = run_bass_kernel_spmd(nc, in_maps, core_ids=list(range(8)))
    return np.concatenate([r["out"] for r in res.results], axis=0)
```

```python
import contextlib
import numpy as np
import concourse.bass as bass
import concourse.mybir as mybir
from concourse.bass_utils import run_bass_kernel_spmd

F32 = mybir.dt.float32
BF16 = mybir.dt.bfloat16
AF = mybir.ActivationFunctionType
ALU = mybir.AluOpType
AX = mybir.AxisListType

S = 2048
D = 1024
NG = 4
NCOLW = 2918
TCOL0 = 2662
NEG = -30000.0

C_IDB, C_TRID, C_TRIA, C_OV, C_SELQ, C_BAND, C_CMB0, C_END = 0, 128, 256, 384, 416, 1184, 1344, 1856
F_N1, F_CONVW, F_CONVB, F_BA, F_BX, F_LAM, F_C8, F_B1, F_PATK, F_PATA, F_PEC, F_ONE, F_HBA, F_HBX, F_C4, F_HB1, F_HALF, F_MHALF, F_END = 0, 8, 24, 28, 32, 36, 40, 44, 48, 112, 176, 208, 216, 220, 224, 228, 232, 233, 240


class Prog:
    ENGS = ("pe", "act", "dve", "pool", "sp")

    def __init__(self, nc):
        self.nc = nc
        self.ops = {e: [] for e in self.ENGS}
        self.W = {}
        self.R = {}
        self.dma_cnt = {}

    def _deps_for(self, eng, reads, writes, is_dma):
        deps = {}

        def add(key, val, kind):
            if key[0] == "e" and key[1] == eng and not is_dma and kind != "raw" and eng == "pe":
                return
            if key not in deps or deps[key] < val:
                deps[key] = val

        for r in reads:
            for key, val in self.W.get(r, {}).items():
                add(key, val, "raw")
        for w in writes:
            for key, val in self.W.get(w, {}).items():
                add(key, val, "waw")
            for key, val in self.R.get(w, {}).items():
                add(key, val, "war")
        return deps

    @staticmethod
    def _l(x):
        return [x] if isinstance(x, str) else list(x)

    def op(self, eng, fn, reads=(), writes=()):
        reads, writes = self._l(reads), self._l(writes)
        deps = self._deps_for(eng, reads, writes, False)
        idx = len(self.ops[eng])
        self.ops[eng].append(dict(fn=fn, deps=deps, dma=None))
        key = ("e", eng)
        for r in reads:
            self.R.setdefault(r, {})[key] = idx
        for w in writes:
            self.R[w] = {}
            self.W.setdefault(w, {})[key] = idx
        return idx

    def dma(self, q, stream, fn, reads=(), writes=()):
        reads, writes = self._l(reads), self._l(writes)
        deps = self._deps_for(q, reads, writes, True)
        self.dma_cnt[stream] = self.dma_cnt.get(stream, 0) + 16
        val = self.dma_cnt[stream]
        self.ops[q].append(dict(fn=fn, deps=deps, dma=stream))
        key = ("d", stream)
        for r in reads:
            self.R.setdefault(r, {})[key] = val
        for w in writes:
            self.R[w] = {}
            self.W.setdefault(w, {})[key] = val
        return val

    def barrier(self):
        deps = {}
        for e in self.ENGS:
            for i in range(len(self.ops[e]) - 1, -1, -1):
                o = self.ops[e][i]
                if o["fn"] is not None and o["dma"] is None:
                    deps[("e", e)] = i
                    break
        for s, v in self.dma_cnt.items():
            deps[("d", s)] = v
        for e in self.ENGS:
            d = {k: v for k, v in deps.items() if k != ("e", e)}
            self.ops[e].append(dict(fn=None, deps=d, dma=None))

    def wait_all(self, eng, resources):
        deps = {}
        for r in resources:
            for key, val in self.W.get(r, {}).items():
                if key not in deps or deps[key] < val:
                    deps[key] = val
        self.ops[eng].append(dict(fn=None, deps=deps, dma=None))

    def emit(self):
        nc = self.nc
        flagged = {e: set() for e in self.ENGS}
        for e in self.ENGS:
            for o in self.ops[e]:
                for key, val in o["deps"].items():
                    if key[0] == "e":
                        flagged[key[1]].add(val)
        cnt = {}
        for e in self.ENGS:
            c = 0
            m = {}
            fl = flagged[e]
            for i in range(len(self.ops[e])):
                if i in fl:
                    c += 1
                m[i] = c
            cnt[e] = m
        self.max_counts = {e: (max(cnt[e].values()) if cnt[e] else 0) for e in self.ENGS}
        sems = {}
        with contextlib.ExitStack() as st:
            for e in self.ENGS:
                sems[("e", e)] = st.enter_context(nc.semaphore("s_" + e))
            for s in self.dma_cnt:
                sems[("d", s)] = st.enter_context(nc.semaphore("d_" + s))
            block = st.enter_context(nc.Block())
            engmap = {"pe": block.tensor, "act": block.scalar, "dve": block.vector,
                      "pool": block.gpsimd, "sp": block.sync}
            for e in self.ENGS:
                ops = self.ops[e]
                if not ops:
                    continue
                fl = flagged[e]

                def body(eng, e=e, ops=ops, fl=fl):
                    seen = {}
                    for i, o in enumerate(ops):
                        for key, val in o["deps"].items():
                            v = cnt[key[1]][val] if key[0] == "e" else val
                            if seen.get(key, 0) >= v:
                                continue
                            seen[key] = v
                            eng.wait_ge(sems[key], v)
                        if o["fn"] is None:
                            continue
                        ins = o["fn"](eng)
                        if o["dma"] is not None:
                            ins.then_inc(sems[("d", o["dma"])], 16)
                        elif i in fl:
                            ins.then_inc(sems[("e", e)], 1)

                engmap[e](body)


class Mem:
    def __init__(self, nc):
        self.nc = nc
        self.base = (nc.sbuf_base + 31) // 32 * 32
        self.top = nc.sbuf_top
        self.cur = self.base
        self.hi = self.base

    def alloc(self, name, shape, dt):
        n = 1
        for s in shape[1:]:
            n *= s
        nbytes = (n * (4 if dt == F32 else 2) + 31) // 32 * 32
        t = self.nc.alloc_sbuf_tensor_at(name, list(shape), dt, offset=self.cur)
        self.cur += nbytes
        self.hi = max(self.hi, self.cur)
        assert self.cur <= self.top, f"SBUF overflow at {name}: {self.cur - self.top} bytes over"
        return t


def build_program(nseq=2, dbg=(), stop_after=None):
    nc = bass.Bass("TRN2", target_bir_lowering=False)
    dram = lambda name, shape, kind="ExternalInput": nc.dram_tensor(name, list(shape), F32, kind=kind).ap()
    x_d = dram("x", [nseq, S, D])
    win_d = dram("w_in", [D, NCOLW])
    wout_d = dram("w_out", [D, D])
    wax_d = dram("wax", [128, 4 * 2 * 128])
    w1_d = dram("w1", [128, 32 * 256])
    w2_d = dram("w2", [128, 2 * 2 * 64])
    nf_d = dram("nf", [D])
    cb_d = dram("cb", [128, C_END])
    cf_d = dram("cf", [128, F_END])
    er_d = dram("erows", [32, S])
    out_d = dram("out", [nseq, S, D], kind="ExternalOutput")
    dbg_out = {}

    P = Prog(nc)
    M = Mem(nc)
    st = contextlib.ExitStack()
    PS = [st.enter_context(nc.psum_tensor(f"ps{b}", [128, 512], F32)) for b in range(7)]
    PSB = st.enter_context(nc.psum_tensor("psb", [128, 8, 128], BF16))

    WIN = M.alloc("WIN", [128, 8, NCOLW], BF16)
    WOUT = M.alloc("WOUT", [128, 8, D], BF16)
    W2 = M.alloc("W2", [128, 2, 2, 64], BF16)
    WAX = M.alloc("WAX", [128, 4, 2, 128], BF16)
    CB = M.alloc("CB", [128, C_END], BF16)
    CF = M.alloc("CF", [128, F_END], F32)
    SMALL = M.alloc("SMALL", [128, 80], F32)
    PEB = M.alloc("PEB", [128, 32], BF16)
    HST = M.alloc("HST", [128, 4], F32)
    IMPT = M.alloc("IMPT", [128, 4, 48], F32)
    MBTOK = M.alloc("MBTOK", [128, 4, 32], BF16)
    RG = M.alloc("RG", [128, 4, S], BF16)
    Q = M.alloc("Q", [128, 4, S], BF16)
    AG = M.alloc("AG", [128, 4, S], BF16)
    KE = M.alloc("KE", [96, 2, S], BF16)
    KW = M.alloc("KW", [128, 2, S], BF16)
    EN = M.alloc("EN", [128, S], F32)
    V = M.alloc("V", [128, 16, 4, 66], BF16)
    KC = M.alloc("KC", [128, 2, 128], BF16)
    VC = M.alloc("VC", [128, 2, 66], BF16)
    ov_base = M.cur

    XT = [M.alloc(f"XT{j}", [128, D], F32) for j in range(2)]
    XS = [M.alloc(f"XS{j}", [128, D], BF16) for j in range(2)]
    HT = [M.alloc(f"HT{j}", [128, 8, 512], BF16) for j in range(2)]
    HIDS = nc.alloc_sbuf_tensor_at("HIDS", [128, 2, 2, 2, 128], BF16, offset=M.cur - 2 * 8192 - 2 * 2048)
    W1 = nc.alloc_sbuf_tensor_at("W1", [128, 32, 256], BF16, offset=M.cur - 2 * 8192)
    RXG = [M.alloc(f"RXG{j}", [128, 4, 515], BF16) for j in range(2)]
    XC = M.alloc("XC", [128, 512], F32)
    XCV = [M.alloc(f"XCV{j}", [128, 512], F32) for j in range(2)]
    XCB = M.alloc("XCB", [128, 512], BF16)
    AA = M.alloc("AA", [128, 512], F32)
    UU = M.alloc("UU", [128, 512], F32)
    TH = M.alloc("TH", [128, 512], BF16)
    KVC = M.alloc("KVC", [128, 2, S], BF16)
    p1_end = M.cur
    M.cur = ov_base
    PT = [M.alloc(f"PT{j}", [128, 512], BF16) for j in range(4)]
    QM = [M.alloc(f"QM{j}", [96, 512], BF16) for j in range(4)]
    OSP = [M.alloc(f"OSP{j}", [128, 3, 512], F32) for j in range(2)]
    OSPC = [M.alloc(f"OSPC{j}", [128, 2, 512], F32) for j in range(2)]
    DENSB = [M.alloc("DENSB", [97, 2, 512], F32)] * 2
    DEN = M.alloc("DEN", [128, 512], F32)
    PRD = M.alloc("PRD", [128, 512], F32)
    SHI = M.alloc("SHI", [128, 512], BF16)
    SLO = M.alloc("SLO", [128, 512], BF16)
    MIXA = M.alloc("MIXA", [128, 4, 512], BF16)
    MBT = M.alloc("MBT", [32, 512], BF16)
    XR = [M.alloc(f"XR{j}", [128, D], F32) for j in range(2)]
    WNF = M.alloc("WNF", [128, D], F32)
    JUNK = M.alloc("JUNK", [128, 512], BF16)
    ov_end = max(M.cur, p1_end)
    M.cur = ov_end

    IDB = CB[:, C_IDB:C_IDB + 128]
    TRID = CB[:, C_TRID:C_TRID + 128]
    TRIA = CB[:, C_TRIA:C_TRIA + 128]

    def dump(name, ap, shape, dt, reads, dst=None):
        if name not in dbg:
            return
        if name not in dbg_out:
            dbg_out[name] = nc.dram_tensor("dbg_" + name, list(shape), dt, kind="ExternalOutput").ap()
        t = dbg_out[name]
        d_ap = t if dst is None else dst(t)
        P.dma("sp", "dbg", lambda e: e.dma_start(out=d_ap, in_=ap), reads=reads, writes="dbg_" + name)

    def mm(out, lhsT, rhs, start, stop, reads, writes, sgc=False):
        P.op("pe", lambda e: e.matmul(out, lhsT=lhsT, rhs=rhs, start=start, stop=stop, skip_group_check=sgc), reads, writes)

    def act(out, in_, func, reads, writes, **kw):
        P.op("act", lambda e: e.activation(out=out, in_=in_, func=func, **kw), reads, writes)

    def vcopy(eng, out, in_, reads, writes):
        P.op(eng, lambda e: e.tensor_copy(out=out, in_=in_), reads, writes)

    def tsc(eng, out, in0, s1, s2, op0, op1, reads, writes):
        if op1 is None:
            P.op(eng, lambda e: e.tensor_scalar(out=out, in0=in0, scalar1=s1, scalar2=None, op0=op0), reads, writes)
        else:
            P.op(eng, lambda e: e.tensor_scalar(out=out, in0=in0, scalar1=s1, scalar2=s2, op0=op0, op1=op1), reads, writes)

    def tt(eng, out, in0, in1, op, reads, writes):
        P.op(eng, lambda e: e.tensor_tensor(out=out, in0=in0, in1=in1, op=op), reads, writes)

    def stt(out, in0, scalar, in1, op0, op1, reads, writes):
        P.op("dve", lambda e: e.scalar_tensor_tensor(out=out, in0=in0, scalar=scalar, in1=in1, op0=op0, op1=op1),
             reads, writes)

    def o_tile(p, br):
        if br == 0:
            g2 = (p // 2) % 2
            return OSPC[g2][:, p % 2, :], f"OSPC{g2}:{p % 2}"
        return OSP[p % 2][:, br, :], f"OSP{p % 2}:{br}"

    def evac_o(osb, p, e, br, ob):
        rows = slice(64 * e, 64 * e + 64)
        if br == 0:
            g2 = (p // 2) % 2
            dst, nm = OSPC[g2][rows, p % 2, :], f"OSPC{g2}:{p % 2}:{e}"
        else:
            dst, nm = OSP[osb][rows, br, :], f"OSP{osb}:{br}:{e}"
        act(dst, PS[ob][0:64, :], AF.Copy, reads=f"ps{ob}", writes=nm)
        dsl = br * 2 + e
        dsp = slice(32 * (dsl % 4), 32 * (dsl % 4) + 1)
        if e == 0:
            act(DENSB[0][dsp, dsl // 4, :], PS[ob][64:65, :], AF.Copy, reads=f"ps{ob}", writes=f"DSB:{dsl}")
        else:
            vcopy("dve", DENSB[0][dsp, dsl // 4, :], PS[ob][64:65, :], reads=f"ps{ob}", writes=f"DSB:{dsl}")
        drow = 32 * p + br * 2 + e
        P.dma("sp", f"den{p}_{br}", lambda en: en.dma_start(out=DEN[drow:drow + 1, :], in_=DENSB[0][dsp, dsl // 4, :]),
              reads=f"DSB:{dsl}", writes=f"DEN{p}")

    bank_rr = {"i": 0}

    P.dma("sp", "c_cf", lambda e: e.dma_start(out=CF[:], in_=cf_d), writes="CF")
    P.dma("pool", "c_cb", lambda e: e.dma_start(out=CB[:], in_=cb_d), writes="CB")
    for g in range(2):
        P.dma("pool", "c_er", (lambda g: lambda e: e.dma_start(out=KE[64:96, g, :], in_=er_d))(g), writes=f"KEE{g}")
    P.dma("pool", "c_wax", lambda e: e.dma_start(out=WAX[:].rearrange("p a b c -> p (a b c)"), in_=wax_d), writes="WAX")
    P.dma("pool", "c_w2", lambda e: e.dma_start(out=W2[:].rearrange("p a b c -> p (a b c)"), in_=w2_d), writes="W2")
    P.op("pool", lambda e: e.memset(V[:, :, :, 64:65], 1.0), writes="Vones")
    P.op("pool", lambda e: e.memset(VC[:, :, 64:65], 1.0), writes="VCones")
    STG0 = nc.alloc_sbuf_tensor_at("STGa", [128, 8, 512], F32, offset=ov_base)
    STG1 = nc.alloc_sbuf_tensor_at("STGb", [128, 8, 512], F32, offset=ov_base + 16384)
    assert ov_base + 32768 <= p1_end
    STGs = [STG0, STG1]
    win_v = win_d.rearrange("(k p) n -> p k n", p=128)
    nblk = (NCOLW + 511) // 512
    for blk in range(nblk):
        c0, c1 = blk * 512, min(NCOLW, blk * 512 + 512)
        sg = STGs[blk % 2]
        P.dma("sp", f"stg{blk % 2}", (lambda sg, c0, c1: lambda e: e.dma_start(out=sg[:, :, 0:c1 - c0], in_=win_v[:, :, c0:c1]))(sg, c0, c1),
              writes=f"STG{blk % 2}")
        for k in range(8):
            eng = "dve" if k % 2 == 0 else "pool"
            tsc(eng, WIN[:, k, c0:c1], sg[:, k, 0:c1 - c0], CF[:, F_N1 + k:F_N1 + k + 1], 0.0, ALU.mult, ALU.add,
                reads=[f"STG{blk % 2}", "CF"], writes=f"WIN{blk}")
    P.dma("pool", "c_wout", lambda e: e.dma_start(out=WOUT[:], in_=wout_d.rearrange("(k p) n -> p k n", p=128)), writes="WOUT")
    EE = SMALL[:, 32:36]
    TT = SMALL[:, 36:40]
    act(EE, CF[:, F_LAM:F_LAM + 4], AF.Exp, reads="CF", writes="EE", scale=-1.0)
    tsc("dve", TT, EE, -0.25, 1.0 / 3.0, ALU.mult, ALU.add, reads="EE", writes="TTs")
    tt("dve", TT, TT, EE, ALU.mult, reads=["TTs", "EE"], writes="TTs")
    tsc("dve", TT, TT, -1.0, 0.5, ALU.mult, ALU.add, reads="TTs", writes="TTs")
    tt("dve", TT, TT, EE, ALU.mult, reads=["TTs", "EE"], writes="TTs")
    tsc("dve", TT, TT, -1.0, 1.0, ALU.mult, ALU.add, reads="TTs", writes="TTs")
    tt("dve", TT, TT, EE, ALU.mult, reads=["TTs", "EE"], writes="TTs")
    tsc("dve", CF[:, F_C8:F_C8 + 4], TT, -8.0, None, ALU.mult, None, reads="TTs", writes="C8")
    tsc("dve", CF[:, F_C4:F_C4 + 4], CF[:, F_C8:F_C8 + 4], 0.5, None, ALU.mult, None, reads="C8", writes="C4")
    tsc("dve", CF[:, F_HBA:F_HBA + 4], CF[:, F_BA:F_BA + 4], 0.5, None, ALU.mult, None, reads="CF", writes="HBA")
    tsc("dve", CF[:, F_HBX:F_HBX + 4], CF[:, F_BX:F_BX + 4], 0.5, None, ALU.mult, None, reads="CF", writes="HBX")
    tsc("dve", WOUT[:].rearrange("p a b -> p (a b)"), WOUT[:].rearrange("p a b -> p (a b)"), 0.5, None, ALU.mult, None, reads="WOUT", writes="WOUT")
    tsc("dve", W2[:].rearrange("p a b c -> p (a b c)"), W2[:].rearrange("p a b c -> p (a b c)"), 0.5, None, ALU.mult, None, reads="W2", writes="W2")
    P.barrier()

    x_t = x_d.rearrange("s (i p) d -> s i p d", p=128)
    out_t = out_d.rearrange("s (i p) d -> s i p d", p=128)

    def chunk_cols(c):
        if c < 20:
            return c * 128, 128
        return 2560, 102

    for s in range(nseq):
        def stage_front(i, s=s):
            j = i % 2
            P.dma("sp", f"xt{j}", (lambda j, i, s: lambda e: e.dma_start(out=XT[j][:], in_=x_t[s, i]))(j, i, s), writes=f"XT{j}")
            ssq = SMALL[:, i:i + 1]
            rs = SMALL[:, 16 + i:16 + i + 1]
            act(XS[j][:], XT[j][:], AF.Square, reads=f"XT{j}", writes=[f"XS{j}", f"ssq{i}"], accum_out=ssq)
            tsc("dve", rs, ssq, 1.0 / D, 1e-6, ALU.mult, ALU.add, reads=f"ssq{i}", writes=f"rs{i}")
            tt("pool", rs, rs, CF[:, F_MHALF:F_MHALF + 1], ALU.pow, reads=[f"rs{i}", "CF"], writes=f"rs{i}")
            tsc("dve", XS[j][:], XT[j][:], rs, None, ALU.mult, None, reads=[f"XT{j}", f"rs{i}"], writes=f"XS{j}")

        def stage_back(i):
            n_, i4 = i // 4, i % 4
            hs = n_ % 2
            j = i % 2
            for k in range(8):
                P.op("pe", (lambda j, k: lambda e: e.transpose(out=PSB[:, k, :], in_=XS[j][:, k * 128:(k + 1) * 128], identity=IDB))(j, k),
                     reads=[f"XS{j}", "CB"], writes="psb")
            act(HT[hs][:, :, i4 * 128:(i4 + 1) * 128], PSB[:], AF.Copy, reads="psb", writes=f"HT{hs}:{i4}")

        def proj_chunk(n, c):
            hs = n % 2
            rb = n % 2
            c_tok = slice(n * 512, (n + 1) * 512)
            c0, cw = chunk_cols(c)
            b = bank_rr["i"] % 7
            bank_rr["i"] += 1
            pb = PS[b]
            htr = [f"HT{hs}:{q}" for q in range(4)]
            for k in range(8):
                mm(pb[0:cw, :], WIN[:, k, c0:c0 + cw], HT[hs][:, k, :], k == 0, k == 7,
                   reads=[f"WIN{c0 // 512}"] + htr, writes=f"ps{b}")
            rd = f"ps{b}"
            if c < 4:
                if c % 2 == 0:
                    act(RXG[rb][:, c, 3:515], pb[:], AF.Copy, reads=rd, writes=f"RX{rb}:{c}")
                else:
                    vcopy("dve", RXG[rb][:, c, 3:515], pb[:], reads=rd, writes=f"RX{rb}:{c}")
            elif c < 8:
                act(TH[:], pb[:], AF.Tanh, reads=rd, writes="TH", scale=0.5)
                stt(RG[:, c - 4, c_tok], TH[:], 1.0, pb[:], ALU.add, ALU.mult, reads=["TH", rd], writes=f"RG{c - 4}:{n}")
            elif c < 12:
                tsc("dve", Q[:, c - 8, c_tok], pb[:], 0.125, None, ALU.mult, None, reads=rd, writes=[f"Q{c - 8}:{n}"])
            elif c < 14:
                vcopy("dve", KVC[:, c - 12, c_tok], pb[:], reads=rd, writes=f"KVC{c - 12}:{n}")
            elif c == 14:
                vcopy("dve", KE[0:64, 0, c_tok], pb[0:64, :], reads=rd, writes=f"KE0:{n}")
                act(KE[0:64, 1, c_tok], pb[64:128, :], AF.Copy, reads=rd, writes=f"KE1:{n}")
            elif c == 15:
                vcopy("dve", KW[0:64, 0, c_tok], pb[0:64, :], reads=rd, writes=f"KW0:{n}")
                act(KW[64:128, 0, c_tok], pb[0:64, :], AF.Copy, reads=rd, writes=f"KW0:{n}")
                vcopy("dve", KW[64:128, 1, c_tok], pb[64:128, :], reads=rd, writes=f"KW1:{n}")
                act(KW[0:64, 1, c_tok], pb[64:128, :], AF.Copy, reads=rd, writes=f"KW1:{n}")
            elif c < 20:
                act(TH[:], pb[:], AF.Tanh, reads=rd, writes="TH", scale=0.5)
                stt(AG[:, c - 16, c_tok], TH[:], 1.0, pb[:], ALU.add, ALU.mult, reads=["TH", rd], writes=[f"AG{c - 16}:{n}"])
            else:
                act(EN[0:102, c_tok], pb[0:102, :], AF.Exp, reads=rd, writes=f"EN:{n}", scale=-1.0)

        def v_part(n):
            hs = n % 2
            for i4 in range(4):
                i = 4 * n + i4
                b = bank_rr["i"] % 7
                bank_rr["i"] += 1
                pb = PS[b]
                for k in range(8):
                    mm(pb[:, 0:256], HT[hs][:, k, i4 * 128:(i4 + 1) * 128], WIN[:, k, TCOL0:TCOL0 + 256], k == 0, k == 7,
                       reads=["WIN5", f"HT{hs}:{i4}"], writes=f"ps{b}")
                vcopy("dve", V[:, i, :, 0:64], pb[:, 0:256].rearrange("p (a b) -> p a b", a=4), reads=f"ps{b}", writes=f"V{i}")

        def halo(n):
            rb = n % 2
            if n == 0:
                P.op("pool", lambda e: e.memset(RXG[0][:, :, 0:3], 0.0), writes=[f"RXh{rb}"])
            else:
                vcopy("pool", RXG[rb][:, :, 0:3], RXG[1 - rb][:, :, 512:515], reads=[f"RX{1 - rb}:{c}" for c in range(4)], writes=f"RXh{rb}")

        def lru_conv(n, c):
            rb = n % 2
            xv = XCV[c % 2]
            rxr = [f"RX{rb}:{c}", f"RXh{rb}"]
            cw = lambda k: CF[:, F_CONVW + 4 * c + k:F_CONVW + 4 * c + k + 1]
            tsc("dve", xv[:], RXG[rb][:, c, 3:515], cw(3), CF[:, F_CONVB + c:F_CONVB + c + 1], ALU.mult, ALU.add,
                reads=rxr + ["CF"], writes=f"XCV{c % 2}")
            for k in (2, 1, 0):
                stt(xv[:], RXG[rb][:, c, k:512 + k], cw(k), xv[:], ALU.mult, ALU.add, reads=rxr + ["CF", f"XCV{c % 2}"], writes=f"XCV{c % 2}")

        def lru_cast(n, c):
            vcopy("pool", XCB[:], XCV[c % 2][:], reads=f"XCV{c % 2}", writes="XCB")

        def lru_a(n, c):
            lru_conv(n, c)
            lru_cast(n, c)

        def lru_b(n, c, s=s):
            c_tok = slice(n * 512, (n + 1) * 512)
            for ai, (dst, boff, nm) in enumerate(((AA, F_HBA, "AA"), (UU, F_HBX, "UU"))):
                b = bank_rr["i"] % 7
                bank_rr["i"] += 1
                mm(PS[b][:, :], WAX[:, c, ai, :], XCB[:], True, True, reads=["WAX", "XCB"], writes=f"ps{b}")
                act(dst[:], PS[b][:, :], AF.Tanh, reads=[f"ps{b}", "HBA", "HBX"], writes=nm, scale=0.5, bias=CF[:, boff + c:boff + c + 1])
            act(AA[:], AA[:], AF.Exp, reads=["AA", "C4"], writes="AA", scale=CF[:, F_C4 + c:F_C4 + c + 1], bias=CF[:, F_C4 + c:F_C4 + c + 1])
            stt(UU[:], UU[:], 1.0, XCV[c % 2][:], ALU.add, ALU.mult, reads=["UU", f"XCV{c % 2}"], writes="UU")
            tt("pool", XC[:], AA[:], AA[:], ALU.mult, reads=["AA"], writes="XC")
            tsc("pool", XC[:], XC[:], -1.0, 1.0, ALU.mult, ALU.add, reads="XC", writes="XC")
            act(XC[:], XC[:], AF.Sqrt, reads="XC", writes="XC")
            stt(UU[:], UU[:], 0.5, XC[:], ALU.mult, ALU.mult, reads=["UU", "XC"], writes="UU")
            init = 0.0 if n == 0 else HST[:, c:c + 1]
            P.op("dve", (lambda init: lambda e: e.tensor_tensor_scan(out=XC[:], data0=AA[:], data1=UU[:], initial=init, op0=ALU.mult, op1=ALU.add))(init),
                 reads=["AA", "UU", f"HST{c}"], writes="XC")
            vcopy("dve", HST[:, c:c + 1], XC[:, 511:512], reads="XC", writes=f"HST{c}")
            if s == 0 and c == 0:
                dump("LRU0", XC[:], [128, S], F32, ["XC"], dst=(lambda c_tok: lambda t: t[:, c_tok])(c_tok))
            tt("dve", RG[:, c, c_tok], XC[:], RG[:, c, c_tok], ALU.mult, reads=["XC", f"RG{c}:{n}"], writes=f"RG{c}:{n}")

        stage_front(0)
        for i in range(4):
            stage_front(i + 1)
            stage_back(i)
        for n in range(NG):
            halo(n)
            for c in range(21):
                proj_chunk(n, c)
                if n > 0:
                    if c == 0:
                        lru_a(n - 1, 0)
                    if c in (3, 8, 13, 18):
                        t_ = (c - 3) // 5
                        if t_ < 3:
                            lru_conv(n - 1, t_ + 1)
                        lru_b(n - 1, t_)
                        if t_ < 3:
                            lru_cast(n - 1, t_ + 1)
                if n < NG - 1 and c in (3, 8, 13, 18):
                    ti = 4 * (n + 1) + (c - 3) // 5
                    stage_back(ti)
                    if ti + 1 < 16:
                        stage_front(ti + 1)
            v_part(n)
        P.dma("pool", "c_w1", lambda e: e.dma_start(out=W1[:].rearrange("p a b -> p (a b)"), in_=w1_d),
              writes=["W1"] + [f"HT{a}:{q}" for a in range(2) for q in range(4)])
        if s == 0:
            vcopy("dve", PEB[:], CF[:, F_PEC:F_PEC + 32], reads="CF", writes="PEB")
            for kv in range(2):
                rows = slice(64 * kv, 64 * kv + 64)
                b = 2 * kv
                for hc in range(2):
                    col = kv * 2 + hc
                    for l in range(32):
                        mm(PS[b][:, col:col + 1], W1[rows, l, hc * 128:(hc + 1) * 128], PEB[rows, l:l + 1], l == 0, l == 31,
                           reads=["W1", "PEB"], writes=f"ps{b}")
                vcopy("dve", CF[:, F_B1 + 2 * kv:F_B1 + 2 * kv + 2], PS[b][:, 2 * kv:2 * kv + 2], reads=f"ps{b}", writes="B1")
                tsc("dve", CF[:, F_HB1 + 2 * kv:F_HB1 + 2 * kv + 2], PS[b][:, 2 * kv:2 * kv + 2], 0.5, None, ALU.mult, None, reads=f"ps{b}", writes="HB1")
        kvc_all = [f"KVC{g}:{n}" for g in range(2) for n in range(NG)]
        blk_i = 0
        for kv in range(2):
            rows = slice(64 * kv, 64 * kv + 64)
            for g in range(2):
                for hc in range(2):
                    b = 2 * kv + hc
                    for l in range(32):
                        mm(PS[b][:, 0:127], W1[rows, l, hc * 128:(hc + 1) * 128], KVC[rows, g, l:l + 2017:16], l == 0, l == 31,
                           reads=["W1"] + kvc_all, writes=f"ps{b}")
                    bcol = kv * 2 + hc
                    act(TH[:, 0:127], PS[b][:, 0:127], AF.Tanh, reads=[f"ps{b}", "HB1"], writes="TH", scale=0.5,
                        bias=CF[:, F_HB1 + bcol:F_HB1 + bcol + 1])
                    tsc("dve", TH[:, 0:127], TH[:, 0:127], 1.0, None, ALU.add, None, reads="TH", writes="TH")
                    stt(HIDS[:, kv, g, hc, 0:127], PS[b][:, 0:127], CF[:, F_B1 + bcol:F_B1 + bcol + 1], TH[:, 0:127], ALU.add, ALU.mult,
                        reads=[f"ps{b}", "B1", "TH"], writes=[f"HIDS{kv}{g}{hc}", "XS0"])
                    if blk_i % 2 == 0:
                        lru_a(NG - 1, blk_i // 2)
                    else:
                        lru_b(NG - 1, blk_i // 2)
                    blk_i += 1
        for g in range(2):
            b = 0
            for hc in range(2):
                mm(PS[b][0:64, 0:127], W2[:, 0, hc, :], HIDS[:, 0, g, hc, 0:127], hc == 0, hc == 1,
                   reads=["W2", f"HIDS0{g}{hc}"], writes=f"ps{b}")
            vcopy("dve", KC[0:64, g, 0:127], PS[b][0:64, 0:127], reads=f"ps{b}", writes=f"KC{g}")
            act(KC[64:128, g, 0:127], PS[b][0:64, 0:127], AF.Copy, reads=f"ps{b}", writes=f"KC{g}")
            b = 1
            for hc in range(2):
                mm(PS[b][0:127, 0:64], HIDS[:, 1, g, hc, 0:127], W2[:, 1, hc, :], hc == 0, hc == 1,
                   reads=["W2", f"HIDS1{g}{hc}"], writes=f"ps{b}")
            vcopy("dve", VC[0:127, g, 0:64], PS[b][0:127, 0:64], reads=f"ps{b}", writes=f"VC{g}")
        P.barrier()
        if s == 0:
            allq = lambda nm, cnt: [f"{nm}{c}:{n}" for c in range(cnt) for n in range(NG)]
            dump("KC", KC[:], [128, 2, 128], BF16, ["KC0", "KC1"])
            dump("VC", VC[:], [128, 2, 66], BF16, ["VC0", "VC1", "VCones"])
            dump("RNN", RG[:], [128, 4, S], BF16, allq("RG", 4))
            dump("Q", Q[:], [128, 4, S], BF16, allq("Q", 4))
            dump("AG", AG[:], [128, 4, S], BF16, allq("AG", 4))
            dump("WOUT", WOUT[:], [128, 8, D], BF16, ["WOUT"])
            dump("V", V[:], [128, 16, 4, 66], BF16, [f"V{i}" for i in range(16)])
        if stop_after == 2:
            break
        P.op("pool", lambda e: e.memset(DEN[:], 1.0), writes=[f"DEN{p}" for p in range(4)])
        P.op("pool", lambda e: e.memset(SHI[:], 0.0), writes=["SHI"] + [f"SHI{p}" for p in range(4)])
        P.op("pool", lambda e: e.memset(SLO[:], 0.0), writes=["SLO"] + [f"SLO{p}" for p in range(4)])
        P.op("pool", lambda e: e.memset(PRD[:], 1.0), writes=[f"PRD{p}" for p in range(4)])
        P.dma("sp", "c_nf", lambda e: e.dma_start(out=WNF[:], in_=nf_d.partition_broadcast(128)), writes="WNF")
        rr = {"s": 0, "o": 0, "pt": 0, "qm": 0}

        def nxt(key, mod):
            v = rr[key] % mod
            rr[key] += 1
            return v

        prev_fns = None
        for n in range(NG):
            T0 = n * 512
            tk = slice(T0, T0 + 512)
            ncv = min(127, 32 * n + 31)
            bs0 = 128 - 32 * n

            def cmp_item(g, r4, n=n, T0=T0, tk=tk, ncv=ncv, bs0=bs0):
                h = 4 * g + r4
                p, e = h // 2, h % 2
                rows = slice(64 * e, 64 * e + 64)
                st_ = {}

                def qk():
                    sb = st_["sb"] = nxt("s", 3)
                    mm(PS[sb][0:ncv, :], KC[rows, g, 0:ncv], Q[rows, p, tk], True, False,
                       reads=[f"KC{g}", f"Q{p}:{n}"], writes=f"ps{sb}")
                    mm(PS[sb][0:ncv, :], CB[rows, C_BAND + bs0:C_BAND + bs0 + ncv], CB[rows, C_CMB0:C_CMB0 + 512], False, True,
                       reads=["CB"], writes=f"ps{sb}")
                    pt = st_["pt"] = nxt("pt", 4)
                    act(PT[pt][0:ncv, :], PS[sb][0:ncv, :], AF.Exp, reads=f"ps{sb}", writes=f"PT{pt}")

                def pv():
                    pt = st_["pt"]
                    ob = 3 + nxt("o", 2)
                    mm(PS[ob][0:65, :], VC[0:ncv, g, 0:65], PT[pt][0:ncv, :], True, True,
                       reads=[f"VC{g}", "VCones", f"PT{pt}"], writes=f"ps{ob}")
                    evac_o(p % 2, p, e, 0, ob)
                    for j in range(4):
                        mm(PS[6][:, (j * 4 + r4) * 32:(j * 4 + r4) * 32 + 32], PT[pt][0:ncv, j * 128:(j + 1) * 128], CB[0:ncv, C_OV:C_OV + 32],
                           True, True, reads=[f"PT{pt}", "CB"], writes="ps6")
                return qk, pv

            def imp_dve(g, n=n):
                ib = 6
                psv = PS[ib][:, :]
                rd = SMALL[:, 52:68]
                P.op("dve", lambda en: en.tensor_reduce(out=rd, in_=psv.rearrange("p (a b) -> p a b", a=16), axis=AX.X, op=ALU.add),
                     reads=f"ps{ib}", writes="rd")
                tsc("dve", rd, rd, 1e-30, None, ALU.max, None, reads="rd", writes="rd")
                P.op("dve", lambda en: en.reciprocal(out=rd, in_=rd), reads="rd", writes="rd")
                for j in range(4):
                    i = 4 * n + j
                    x0 = 32 - 2 * i
                    blkv = PS[ib][:, j * 128:(j + 1) * 128]
                    im = IMPT[:, j, 0:32]
                    tsc("dve", im, blkv[:, 0:32], rd[:, 4 * j:4 * j + 1], None, ALU.mult, None, reads=[f"ps{ib}", "rd"], writes=f"im{j}")
                    for r4 in range(1, 4):
                        stt(im, blkv[:, 32 * r4:32 * r4 + 32], rd[:, 4 * j + r4:4 * j + r4 + 1], im, ALU.mult, ALU.add,
                            reads=[f"ps{ib}", "rd", f"im{j}"], writes=f"im{j}")
                    if s == 0:
                        dump("IMP", im, [128, 16, 2, 32], F32, [f"im{j}"], dst=(lambda i, g: lambda t: t[:, i, g, :])(i, g))
                    tt("dve", im, im, CF[:, F_PATK + x0:F_PATK + x0 + 32], ALU.mult, reads=[f"im{j}", "CF"], writes=f"im{j}")
                    tt("dve", im, im, CF[:, F_PATA + x0:F_PATA + x0 + 32], ALU.add, reads=[f"im{j}", "CF"], writes=f"im{j}")
                    P.op("dve", (lambda j: lambda en: en.memset(IMPT[:, j, 0:1], 1e9))(j), reads=f"im{j}", writes=f"im{j}")
                    t8 = IMPT[:, j, 32:40]
                    P.op("dve", (lambda t8, im: lambda en: en.max(out=t8, in_=im))(t8, im), reads=f"im{j}", writes=f"t8{j}")
                    thr = IMPT[:, j, 40:41]
                    tsc("dve", thr, t8[:, 7:8], -5e8, None, ALU.max, None, reads=f"t8{j}", writes=f"thr{j}")
                    tsc("dve", MBTOK[:, j, :], im, thr, NEG, ALU.is_lt, ALU.mult, reads=[f"im{j}", f"thr{j}"], writes=f"MBTOK{j}")

            def mbt_pe(g, n=n, tk=tk):
                for j in range(4):
                    P.op("pe", (lambda j: lambda en: en.transpose(out=PSB[0:32, j, :], in_=MBTOK[:, j, :], identity=IDB))(j),
                         reads=[f"MBTOK{j}", "CB"], writes="psb")
                vcopy("dve", MBT[0:32, :], PSB[0:32, 0:4, :], reads="psb", writes="MBT")
                for p_ in (2 * g, 2 * g + 1):
                    for e_ in range(2):
                        qm = 2 * (p_ % 2) + e_
                        vcopy("pool" if e_ == 0 else "dve", QM[qm][0:64, :], Q[64 * e_:64 * e_ + 64, p_, tk], reads=f"Q{p_}:{n}", writes=f"QM{qm}")
                        vcopy("dve", QM[qm][64:96, :], MBT[0:32, :], reads="MBT", writes=f"QM{qm}")

            def sel_items(p, e, n=n, T0=T0, tk=tk):
                g = p // 2
                rows = slice(64 * e, 64 * e + 64)
                nkc = 4 * n + 4
                st_ = {}
                items = []
                for kc in range(nkc):
                    c0 = max(0, 128 * kc - T0)
                    diag = kc >= 4 * n
                    it = {}

                    def qk(kc=kc, c0=c0, diag=diag, it=it):
                        qm = 2 * (p % 2) + e
                        sb = nxt("s", 3)
                        mm(PS[sb][:, c0:512], KE[0:96, g, kc * 128:(kc + 1) * 128], QM[qm][0:96, c0:512], True, not diag,
                           reads=[f"KE{g}:{kc // 4}", f"KEE{g}", f"QM{qm}"], writes=f"ps{sb}")
                        if diag:
                            mm(PS[sb][:, c0:c0 + 128], IDB, TRID, False, True, reads=["CB"], writes=f"ps{sb}")
                        pt = it["pt"] = nxt("pt", 4)
                        act(PT[pt][:, c0:512], PS[sb][:, c0:512], AF.Exp, reads=f"ps{sb}", writes=f"PT{pt}")

                    def pv(kc=kc, c0=c0, it=it):
                        if kc == 0:
                            st_["ob"] = 3 + nxt("o", 2)
                        ob = st_["ob"]
                        pt = it["pt"]
                        mm(PS[ob][0:65, c0:512], V[:, kc, g, 0:65], PT[pt][:, c0:512], kc == 0, kc == nkc - 1,
                           reads=[f"V{kc}", "Vones", f"PT{pt}"], writes=f"ps{ob}")
                        if kc == nkc - 1:
                            evac_o(p % 2, p, e, 1, ob)
                    items.append((qk, pv))
                return items

            def win_items(p, e, n=n, T0=T0):
                g = p // 2
                rows = slice(64 * e, 64 * e + 64)
                kcs = list(range(max(0, 4 * n - 4), 4 * n + 4))
                st_ = {}
                items = []
                for kc in kcs:
                    i_lo, i_hi = max(kc, 4 * n), min(kc + 4, 4 * n + 3)
                    c0, c1 = (i_lo - 4 * n) * 128, (i_hi - 4 * n + 1) * 128
                    diag = kc >= 4 * n
                    anti = (kc + 4 >= 4 * n) and (kc + 4 <= 4 * n + 3)
                    it = {}

                    def qk(kc=kc, c0=c0, c1=c1, diag=diag, anti=anti, it=it):
                        sb = nxt("s", 3)
                        mm(PS[sb][:, c0:c1], KW[rows, g, kc * 128:(kc + 1) * 128], Q[rows, p, T0 + c0:T0 + c1], True, not (diag or anti),
                           reads=[f"KW{g}:{kc // 4}", f"Q{p}:{n}"], writes=f"ps{sb}")
                        if diag:
                            mm(PS[sb][:, c0:c0 + 128], IDB, TRID, False, True, reads=["CB"], writes=f"ps{sb}")
                        if anti:
                            mm(PS[sb][:, c1 - 128:c1], IDB, TRIA, False, True, reads=["CB"], writes=f"ps{sb}")
                        pt = it["pt"] = nxt("pt", 4)
                        act(PT[pt][:, c0:c1], PS[sb][:, c0:c1], AF.Exp, reads=f"ps{sb}", writes=f"PT{pt}")

                    def pv(kc=kc, c0=c0, c1=c1, it=it):
                        if kc == kcs[0]:
                            st_["ob"] = 3 + nxt("o", 2)
                        ob = st_["ob"]
                        pt = it["pt"]
                        mm(PS[ob][0:65, c0:c1], V[:, kc, 2 + g, 0:65], PT[pt][:, c0:c1], kc == kcs[0], kc == kcs[-1],
                           reads=[f"V{kc}", "Vones", f"PT{pt}"], writes=f"ps{ob}", sgc=True)
                        if kc == kcs[-1]:
                            evac_o(p % 2, p, e, 2, ob)
                    items.append((qk, pv))
                return items

            def run_stream(items, look=2):
                pend = []
                for it in items:
                    if len(it) > 2:
                        for q_ in pend:
                            q_[1]()
                        pend = []
                    if it[0] is not None:
                        it[0]()
                    if it[1] is not None:
                        pend.append(it)
                    while len(pend) > look:
                        pend.pop(0)[1]()
                for q_ in pend:
                    q_[1]()

            def combine_front(p, n=n, T0=T0, tk=tk):
                win = slice(32 * p, 32 * p + 6)
                tsc("pool", DEN[win, :], DEN[win, :], 3e38, 1e-18, ALU.min, ALU.max, reads=f"DEN{p}", writes=f"DEN{p}")
                stt(PRD[win, :], EN[win, tk], 1.0, DEN[win, :], ALU.add, ALU.mult, reads=[f"EN:{n}", f"DEN{p}"], writes=f"PRD{p}")
                P.op("dve", (lambda win: lambda en: en.reciprocal(out=PRD[win, :], in_=PRD[win, :]))(win), reads=[f"PRD{p}"], writes=[f"PRD{p}"])
                vcopy("dve", SHI[win, :], PRD[win, :], reads=f"PRD{p}", writes=f"SHI{p}")
                tt("dve", SLO[win, :], PRD[win, :], SHI[win, :], ALU.subtract, reads=[f"PRD{p}", f"SHI{p}"], writes=f"SLO{p}")
                if s == 0 and n == 0:
                    dump("SG", PRD[win, :], [128, 512], F32, [f"PRD{p}"], dst=(lambda win: lambda t: t[win, :])(win))

            def combine(p, n=n, T0=T0, tk=tk):
                osb = p % 2
                win = slice(32 * p, 32 * p + 6)
                if p < 3:
                    kw_, sset = win, 0
                else:
                    kw_, sset = slice(64, 102), 1
                for br in range(3):
                    bb = 5 + (br % 2)
                    ot, onm = o_tile(p, br)
                    ors = [onm + ":0", onm + ":1"]
                    sel_ap = CB[kw_, C_SELQ + (sset * 3 + br) * 128:C_SELQ + (sset * 3 + br) * 128 + 128]
                    mm(PS[bb][:, :], sel_ap, SHI[kw_, :], True, False, reads=["CB", f"SHI{p}", "SHI"], writes=f"ps{bb}")
                    mm(PS[bb][:, :], sel_ap, SLO[kw_, :], False, True, reads=["CB", f"SLO{p}", "SLO"], writes=f"ps{bb}")
                    tt("dve", ot, ot, PS[bb][:, :], ALU.mult, reads=ors + [f"ps{bb}"], writes=ors)
                    if s == 0:
                        dump("OBR", ot, [128, 3, 4, S], F32, ors, dst=(lambda br, p, tk: lambda t: t[:, br, p, tk])(br, p, tk))
                t0, n0 = o_tile(p, 0)
                t1, n1 = o_tile(p, 1)
                t2, n2 = o_tile(p, 2)
                r0, r1, r2 = [n0 + ":0", n0 + ":1"], [n1 + ":0", n1 + ":1"], [n2 + ":0", n2 + ":1"]
                tt("pool", t1, t1, t2, ALU.add, reads=r1 + r2, writes=r1)
                tt("pool", t1, t1, t0, ALU.add, reads=r1 + r0, writes=r1)
                if s == 0:
                    dump("ATT", t1, [128, 4, S], F32, r1, dst=(lambda p, tk: lambda t: t[:, p, tk])(p, tk))
                tt("pool", MIXA[:, p, :], t1, AG[:, p, tk], ALU.mult, reads=r1 + [f"AG{p}:{n}"], writes=f"MIXA{p}")
                if s == 0:
                    dump("MIXA", MIXA[:, p, :], [128, 4, S], BF16, [f"MIXA{p}"], dst=(lambda p, tk: lambda t: t[:, p, tk])(p, tk))

            def xr_load(j, n=n):
                i = 4 * n + j
                xb = i % 2
                P.dma("sp", f"xr{xb}", (lambda xb, i, s: lambda en: en.dma_start(out=XR[xb][:], in_=x_t[s, i]))(xb, i, s), writes=f"XR{xb}")

            def outproj_tile(j, n=n, T0=T0, xr_load_=xr_load):
                i = 4 * n + j
                xb = i % 2
                if s == 0:
                    dump("XRIN", XR[xb][:], [16, 128, D], F32, [f"XR{xb}"], dst=(lambda i: lambda t: t[i])(i))
                for half in range(2):
                    yb = 5 + half
                    for k in range(8):
                        if k < 4:
                            lhs = RG[:, k, T0 + j * 128:T0 + (j + 1) * 128]
                            rdn = [f"RG{k}:{n}"]
                        else:
                            lhs = MIXA[:, k - 4, j * 128:(j + 1) * 128]
                            rdn = [f"MIXA{k - 4}"]
                        mm(PS[yb][:, :], lhs, WOUT[:, k, half * 512:(half + 1) * 512], k == 0, k == 7,
                           reads=rdn + ["WOUT"], writes=f"ps{yb}")
                    tt("dve", XR[xb][:, half * 512:(half + 1) * 512], XR[xb][:, half * 512:(half + 1) * 512], PS[yb][:, :], ALU.add,
                       reads=[f"XR{xb}", f"ps{yb}"], writes=f"XR{xb}")
                ssq = SMALL[:, 44 + xb:45 + xb]
                rs = SMALL[:, 46 + xb:47 + xb]
                if s == 0:
                    dump("Y", XR[xb][:], [16, 128, D], F32, [f"XR{xb}"], dst=(lambda i: lambda t: t[i])(i))
                ssq2 = SMALL[:, 48 + xb:49 + xb]
                act(JUNK[:], XR[xb][:, 0:512], AF.Square, reads=f"XR{xb}", writes=["JUNK", f"fssq{xb}"], accum_out=ssq)
                act(JUNK[:], XR[xb][:, 512:1024], AF.Square, reads=f"XR{xb}", writes=["JUNK", f"fssqb{xb}"], accum_out=ssq2)
                tt("dve", ssq, ssq, ssq2, ALU.add, reads=[f"fssq{xb}", f"fssqb{xb}"], writes=f"fssq{xb}")
                tsc("dve", rs, ssq, 1.0 / D, 1e-6, ALU.mult, ALU.add, reads=f"fssq{xb}", writes=f"frs{xb}")
                tt("pool", rs, rs, CF[:, F_MHALF:F_MHALF + 1], ALU.pow, reads=[f"frs{xb}", "CF"], writes=f"frs{xb}")
                if s == 0:
                    dump("RSF", SMALL[:, 44:48], [16, 128, 4], F32, [f"frs{xb}", f"fssq{xb}"], dst=(lambda i: lambda t: t[i])(i))
                stt(XR[xb][:], XR[xb][:], rs, WNF[:], ALU.mult, ALU.mult, reads=[f"XR{xb}", f"frs{xb}", "WNF"], writes=f"XR{xb}")
                P.dma("sp", f"st{xb}", (lambda xb, i, s: lambda en: en.dma_start(out=out_t[s, i], in_=XR[xb][:]))(xb, i, s), reads=f"XR{xb}", writes="OUT")
                if j < 2:
                    xr_load_(j + 2)

            def special(fn):
                return (fn, None)

            def with_front(sel_list, nxt_list, p):
                k, real = 0, 0
                while k < len(nxt_list) and real < 3:
                    if nxt_list[k][1] is not None:
                        real += 1
                    k += 1
                return sel_list + nxt_list[:k] + [special((lambda p=p: combine_front(p)))] + nxt_list[k:]

            def real_split(lst, k):
                i_, real = 0, 0
                while i_ < len(lst) and real < k:
                    if lst[i_][1] is not None:
                        real += 1
                    i_ += 1
                return i_

            items = []
            has_prev = prev_fns is not None
            items += [cmp_item(0, r4) for r4 in range(4)]
            items.append(((lambda: imp_dve(0)), None, True))
            w0 = win_items(0, 0) + win_items(0, 1)
            pre = [special(prev_fns["front3"]), special(prev_fns["xr01"])] if has_prev else []
            items += w0[:3] + pre + w0[3:]
            w1a = win_items(1, 0)
            w1 = w1a + win_items(1, 1)
            k6 = real_split(w1, min(6, len(w1a) - 1))
            items += w1[:k6] + ([special(prev_fns["combine3"])] if has_prev else []) + w1[k6:]
            items.append(special(lambda: mbt_pe(0)))
            items += sel_items(0, 0) + sel_items(0, 1)
            sel1 = sel_items(1, 0) + sel_items(1, 1)
            outp = [special((lambda f, j: lambda: f(j))(prev_fns["outproj"], j)) for j in range(4)] if has_prev else []
            items += sel1[:3] + [special(lambda: combine_front(0))] + outp + sel1[3:]
            items.append(special(lambda: combine(0)))
            c1 = [cmp_item(1, r4) for r4 in range(4)]
            items += c1[:3] + [special(lambda: combine_front(1))] + c1[3:]
            items.append(((lambda: imp_dve(1)), None, True))
            items += win_items(2, 0) + win_items(2, 1)
            w3a = win_items(3, 0)
            w3 = w3a + win_items(3, 1)
            k6 = real_split(w3, min(6, len(w3a) - 1))
            items += w3[:k6] + [special(lambda: combine(1))] + w3[k6:]
            items.append(special(lambda: mbt_pe(1)))
            items += sel_items(2, 0) + sel_items(2, 1)
            sel3 = sel_items(3, 0) + sel_items(3, 1)
            items += sel3[:3] + [special(lambda: combine_front(2))] + sel3[3:]
            items.append(special(lambda: combine(2)))
            run_stream(items)
            prev_fns = {"front3": (lambda c=combine_front: c(3)), "combine3": (lambda c=combine: c(3)), "outproj": outproj_tile,
                        "xr01": (lambda f=xr_load: (f(0), f(1)))}
        prev_fns["front3"]()
        prev_fns["xr01"]()
        prev_fns["combine3"]()
        for j in range(4):
            prev_fns["outproj"](j)
        prev_fns = None
        P.barrier()
        if stop_after == 3:
            break

    P.wait_all("sp", ["OUT"] + ["dbg_" + k for k in dbg_out])
    P.emit()
    st.close()
    return nc, P, dbg_out


def host_layout(inp):
    f = lambda k: np.asarray(inp[k], np.float32)
    w_in = f("w_in")[0]
    cols = []
    cols += list(range(0, 1536))
    cols += list(range(1536, 1600)) + list(range(1664, 1728))
    cols += list(range(1600, 1664)) + list(range(1728, 1792))
    cols += list(range(1792, 1920))
    cols += list(range(2048, 2176))
    cols += list(range(2304, 2816))
    w_r = np.zeros((D, NCOLW), np.float32)
    w_r[:, 0:2560] = w_in[:, cols]
    for p in range(4):
        for br in range(3):
            for e in range(2):
                w_r[:, 2560 + 32 * p + br * 2 + e] = w_in[:, 2816 + br * 8 + 2 * p + e]
    w_r[:, TCOL0:TCOL0 + 128] = w_in[:, 1920:2048]
    w_r[:, TCOL0 + 128:TCOL0 + 256] = w_in[:, 2176:2304]
    wax = np.zeros((128, 4, 2, 128), np.float32)
    for ai, key in enumerate(("rg_wa", "rg_wx")):
        w = f(key)[0]
        for c in range(4):
            for hh in range(2):
                wax[hh * 64:(hh + 1) * 64, c, ai, hh * 64:(hh + 1) * 64] = w[2 * c + hh]
    w1 = np.zeros((128, 32, 256), np.float32)
    w1[0:64] = f("cmp_k_w1")[0].reshape(32, 64, 256).transpose(1, 0, 2)
    w1[64:128] = f("cmp_v_w1")[0].reshape(32, 64, 256).transpose(1, 0, 2)
    w2 = np.zeros((128, 2, 2, 64), np.float32)
    w2[:, 0] = f("cmp_k_w2")[0].reshape(2, 128, 64).transpose(1, 0, 2)
    w2[:, 1] = f("cmp_v_w2")[0].reshape(2, 128, 64).transpose(1, 0, 2)
    cf = np.zeros((128, F_END), np.float32)
    cf[:, F_N1:F_N1 + 8] = f("norm1_w")[0].reshape(8, 128).T
    cf[:, F_CONVW:F_CONVW + 16] = f("conv_w")[0].T.reshape(4, 128, 4).transpose(1, 0, 2).reshape(128, 16)
    for off, key in ((F_CONVB, "conv_b"), (F_BA, "rg_ba"), (F_BX, "rg_bx"), (F_LAM, "rg_lambda")):
        cf[:, off:off + 4] = f(key)[0].reshape(4, 128).T
    tl = np.arange(128)[:, None]
    xx = np.arange(64)[None, :]
    jj = xx - 32
    curp = (tl >= 64).astype(np.int64)
    noncausal = jj > curp
    forced = (jj == curp) | (jj == curp - 1)
    cf[:, F_PATK:F_PATK + 64] = np.where(noncausal | forced, 0.0, 1.0)
    cf[:, F_PATA:F_PATA + 64] = np.where(noncausal, -1e9, np.where(forced, 1e9, 0.0))
    cf[0:64, F_PEC:F_PEC + 32] = f("cmp_k_pe")[0].T
    cf[64:128, F_PEC:F_PEC + 32] = f("cmp_v_pe")[0].T
    cf[:, F_ONE] = 1.0
    cf[:, F_HALF] = 0.5
    cf[:, F_MHALF] = -0.5
    cb = np.zeros((128, C_END), np.float32)
    cb[:, C_IDB:C_IDB + 128] = np.eye(128)
    kl = np.arange(128)[:, None]
    ql = np.arange(128)[None, :]
    cb[:, C_TRID:C_TRID + 128] = np.where(kl > ql, NEG, 0.0)
    cb[:, C_TRIA:C_TRIA + 128] = np.where(kl <= ql, NEG, 0.0)
    cs = np.arange(127)[:, None] * 16
    ss = np.arange(32)[None, :] * 64
    ov = np.clip(np.minimum(cs + 32, ss + 64) - np.maximum(cs, ss), 0, None).astype(np.float32) / 32.0
    cb[0:127, C_OV:C_OV + 32] = ov
    selq = np.zeros((128, 2, 3, 128), np.float32)
    for p in range(4):
        for br in range(3):
            for e in range(2):
                selq[32 * p + br * 2 + e, 0, br, 64 * e:64 * e + 64] = 1.0
                if p == 3:
                    selq[32 * p + br * 2 + e, 1, br, 64 * e:64 * e + 64] = 1.0
    cb[:, C_SELQ:C_SELQ + 768] = selq.reshape(128, 768)
    band = np.zeros((32, 160), np.float32)
    for r in range(32):
        band[r, r + 127] = 1.0
    cb[0:32, C_BAND:C_BAND + 160] = band
    cb[64:96, C_BAND:C_BAND + 160] = band
    r_ = np.arange(32)[:, None]
    t_ = np.arange(512)[None, :]
    cb[0:32, C_CMB0:C_CMB0 + 512] = np.where(16 * (r_ - 1) + 31 > t_, NEG, 0.0)
    cb[64:96, C_CMB0:C_CMB0 + 512] = cb[0:32, C_CMB0:C_CMB0 + 512]
    erows = (np.arange(S)[None, :] // 64 == np.arange(32)[:, None]).astype(np.float32)
    return dict(w_in=w_r, w_out=np.ascontiguousarray(f("w_out")[0]), wax=wax.reshape(128, -1), w1=w1.reshape(128, -1),
                w2=w2.reshape(128, -1), nf=f("normf_w"), cb=cb, cf=cf, erows=erows)


_CACHE = {}


def kernel(**inputs):
    x = np.ascontiguousarray(np.asarray(inputs["x"], np.float32))
    shared = host_layout(inputs)
    if "nc" not in _CACHE:
        _CACHE["nc"] = build_program()[0]
    nc = _CACHE["nc"]
    in_maps = [dict(shared, x=x[2 * c:2 * c + 2]) for c in range(8)]
    res = run_bass_kernel_spmd(nc, in_maps, core_ids=list(range(8)))
    return np.concatenate([r["out"] for r in res.results], axis=0)
```
